# Optimizing a Trainium2 kernel written in Bass

```python
import jax, jax.numpy as jnp
from jax import lax
import numpy as np

D_MODEL = 2048
BATCH = 1
SEQ = 8192
DEPTH = 1
DEC_BATCH = 4
DEC_SEQ = 2048
PAST_LEN = 128

HEAD_DIM = 128
N_HEADS_A = 8
N_KV_A = 2
N_HEADS_B = 8
D_A = N_HEADS_A * HEAD_DIM
D_KV_A = N_KV_A * HEAD_DIM
D_B = N_HEADS_B * HEAD_DIM
D_MIX = D_A + D_B
D_IN = D_A + 2 * D_KV_A + D_A + 4 * D_B
N_META = 16
GRID_W = 64
WIN_R_MAX = 8
WIN_C = 16
Q_BLOCK = 128
ROPE_THETA = 10000.0
EPS = 1e-6

kernel_name = "hymba_gqa_natten_encoder"


def rms_norm(x, w):
    xf = x.astype(jnp.float32)
    y = xf * lax.rsqrt(jnp.mean(xf * xf, axis=-1, keepdims=True) + EPS)
    return (y * w.astype(jnp.float32)).astype(x.dtype)


def split_points():
    sizes = [D_A, D_KV_A, D_KV_A, D_A, D_B, D_B, D_B, D_B]
    pts, acc = [], 0
    for s in sizes[:-1]:
        acc += s
        pts.append(acc)
    return pts


def axial_rope_tables(n_tok):
    t = jnp.arange(n_tok, dtype=jnp.int32)
    row = jnp.concatenate([jnp.full((N_META,), -1, jnp.int32), t // GRID_W]).astype(jnp.float32)
    col = jnp.concatenate([jnp.arange(N_META, dtype=jnp.int32), t % GRID_W]).astype(jnp.float32)
    half = HEAD_DIM // 2
    inv_freq = ROPE_THETA ** (-jnp.arange(0, half, 2, dtype=jnp.float32) / half)
    ang_r = row[:, None] * inv_freq[None, :]
    ang_c = col[:, None] * inv_freq[None, :]
    return (jnp.cos(ang_r), jnp.sin(ang_r), jnp.cos(ang_c), jnp.sin(ang_c))


def _rotate(x, cos, sin):
    x1, x2 = jnp.split(x, 2, axis=-1)
    c = cos[None, :, None, :]
    s = sin[None, :, None, :]
    return jnp.concatenate([x1 * c - x2 * s, x1 * s + x2 * c], axis=-1)


def apply_axial_rope(x, tabs):
    cr, sr, cc, sc = tabs
    xr, xc = jnp.split(x.astype(jnp.float32), 2, axis=-1)
    return jnp.concatenate([_rotate(xr, cr, sr), _rotate(xc, cc, sc)], axis=-1).astype(x.dtype)


def global_gqa(q, k, v):
    B, N = q.shape[0], q.shape[1]
    G = N_HEADS_A // N_KV_A
    scale = HEAD_DIM ** -0.5

    def attend(qb):
        L = qb.shape[1]
        qg = qb.reshape(B, L, N_KV_A, G, HEAD_DIM)
        s = jnp.einsum('blkgd,bskd->bkgls', qg, k).astype(jnp.float32) * scale
        p = jax.nn.softmax(s, axis=-1).astype(v.dtype)
        o = jnp.einsum('bkgls,bskd->blkgd', p, v)
        return o.reshape(B, L, D_A)

    o_meta = attend(q[:, :N_META])
    S = N - N_META
    nb = S // Q_BLOCK
    q_blocks = q[:, N_META:].reshape(B, nb, Q_BLOCK, N_HEADS_A, HEAD_DIM).transpose(1, 0, 2, 3, 4)
    o_real = lax.map(attend, q_blocks)
    o_real = o_real.transpose(1, 0, 2, 3).reshape(B, S, D_A)
    return jnp.concatenate([o_meta, o_real], axis=1)


def neighbourhood_tables(n_tok):
    rows = n_tok // GRID_W
    kr = min(WIN_R_MAX, rows)
    t = jnp.arange(n_tok, dtype=jnp.int32)
    r = t // GRID_W
    c = t % GRID_W
    r0 = jnp.clip(r - kr // 2, 0, rows - kr)
    c0 = jnp.clip(c - WIN_C // 2, 0, GRID_W - WIN_C)
    kr_idx = r0[:, None] + jnp.arange(kr, dtype=jnp.int32)[None, :]
    kc_idx = c0[:, None] + jnp.arange(WIN_C, dtype=jnp.int32)[None, :]
    key_idx = (kr_idx[:, :, None] * GRID_W + kc_idx[:, None, :]).reshape(n_tok, kr * WIN_C)
    off_r = kr_idx - r[:, None] + (WIN_R_MAX - 1)
    off_c = kc_idx - c[:, None] + (WIN_C - 1)
    bias_idx = (off_r[:, :, None] * (2 * WIN_C - 1) + off_c[:, None, :]).reshape(n_tok, kr * WIN_C)
    return key_idx, bias_idx


def neighbourhood_attention(q, k, v, rpb):
    B, N, H, Dh = q.shape
    S = N - N_META
    key_idx, bias_idx = neighbourhood_tables(S)
    W = key_idx.shape[1]
    rpb_flat = rpb.reshape(H, -1).astype(jnp.float32)
    scale = Dh ** -0.5
    qm, km, vm = q[:, :N_META], k[:, :N_META], v[:, :N_META]
    kx, vx = k[:, N_META:], v[:, N_META:]
    sm = jnp.einsum('bqhd,bkhd->bhqk', qm, km).astype(jnp.float32) * scale
    o_meta = jnp.einsum('bhqk,bkhd->bqhd', jax.nn.softmax(sm, axis=-1).astype(v.dtype), vm)

    def attend(args):
        qb, kidx, bidx = args
        kg = kx[:, kidx]
        vg = vx[:, kidx]
        s_win = jnp.einsum('bqhd,bqwhd->bhqw', qb, kg).astype(jnp.float32) * scale + rpb_flat[:, bidx][None]
        s_meta = jnp.einsum('bqhd,bkhd->bhqk', qb, km).astype(jnp.float32) * scale
        p = jax.nn.softmax(jnp.concatenate([s_meta, s_win], axis=-1), axis=-1).astype(v.dtype)
        return (jnp.einsum('bhqk,bkhd->bqhd', p[..., :N_META], vm)
                + jnp.einsum('bhqw,bqwhd->bqhd', p[..., N_META:], vg))

    nb = S // Q_BLOCK
    q_blocks = q[:, N_META:].reshape(B, nb, Q_BLOCK, H, Dh).transpose(1, 0, 2, 3, 4)
    o_real = lax.map(attend, (q_blocks, key_idx.reshape(nb, Q_BLOCK, W), bias_idx.reshape(nb, Q_BLOCK, W)))
    o_real = o_real.transpose(1, 0, 2, 3, 4).reshape(B, S, H * Dh)
    return jnp.concatenate([o_meta.reshape(B, N_META, H * Dh), o_real], axis=1)


def encoder_layer(h, norm_w, w_in, q_norm_a, k_norm_a, q_norm_b, k_norm_b, rpb, w_out, rope_tabs):
    B, N, _ = h.shape
    u = rms_norm(h, norm_w) @ w_in.astype(h.dtype)
    qa, ka, va, za, qb, kb, vb, zb = jnp.split(u, split_points(), axis=-1)
    qa = apply_axial_rope(rms_norm(qa.reshape(B, N, N_HEADS_A, HEAD_DIM), q_norm_a), rope_tabs)
    ka = apply_axial_rope(rms_norm(ka.reshape(B, N, N_KV_A, HEAD_DIM), k_norm_a), rope_tabs)
    va = va.reshape(B, N, N_KV_A, HEAD_DIM)
    ya = global_gqa(qa, ka, va) * jax.nn.silu(za)
    qb = rms_norm(qb.reshape(B, N, N_HEADS_B, HEAD_DIM), q_norm_b)
    kb = rms_norm(kb.reshape(B, N, N_HEADS_B, HEAD_DIM), k_norm_b)
    vb = vb.reshape(B, N, N_HEADS_B, HEAD_DIM)
    yb = neighbourhood_attention(qb, kb, vb, rpb) * jax.nn.silu(zb)
    y = jnp.concatenate([ya, yb], axis=-1) @ w_out.astype(h.dtype)
    return h + y


def setup_inputs(seed: int = 0) -> dict:
    key = jax.random.key(seed)
    ks = jax.random.split(key, 12)
    f32 = jnp.float32
    return {
        "x_prompt": jax.random.normal(ks[0], (BATCH, SEQ, D_MODEL), f32),
        "x_sample": jax.random.normal(ks[1], (DEC_BATCH, DEC_SEQ, D_MODEL), f32),
        "meta_tokens": jax.random.normal(ks[2], (N_META, D_MODEL), f32),
        "norm_w": 1.0 + 0.02 * jax.random.normal(ks[3], (DEPTH, D_MODEL), f32),
        "w_in": jax.random.normal(ks[4], (DEPTH, D_MODEL, D_IN), f32) * D_MODEL ** -0.5,
        "q_norm_a": 1.0 + 0.02 * jax.random.normal(ks[5], (DEPTH, HEAD_DIM), f32),
        "k_norm_a": 1.0 + 0.02 * jax.random.normal(ks[6], (DEPTH, HEAD_DIM), f32),
        "q_norm_b": 1.0 + 0.02 * jax.random.normal(ks[7], (DEPTH, HEAD_DIM), f32),
        "k_norm_b": 1.0 + 0.02 * jax.random.normal(ks[8], (DEPTH, HEAD_DIM), f32),
        "rpb": 0.1 * jax.random.normal(ks[9], (DEPTH, N_HEADS_B, 2 * WIN_R_MAX - 1, 2 * WIN_C - 1), f32),
        "w_out": jax.random.normal(ks[10], (DEPTH, D_MIX, D_MODEL), f32) * D_MIX ** -0.5,
    }


def reference(x_prompt, x_sample, meta_tokens, norm_w, w_in, q_norm_a, k_norm_a, q_norm_b, k_norm_b, rpb, w_out):
    def encode(x):
        B, S, _ = x.shape
        meta = jnp.broadcast_to(meta_tokens.astype(x.dtype)[None], (B, N_META, D_MODEL))
        h = jnp.concatenate([meta, x], axis=1)
        tabs = axial_rope_tables(S)
        for l in range(DEPTH):
            h = encoder_layer(h, norm_w[l], w_in[l], q_norm_a[l], k_norm_a[l],
                              q_norm_b[l], k_norm_b[l], rpb[l], w_out[l], tabs)
        return h[:, N_META:]

    y_prompt = encode(x_prompt)
    y_sample = encode(x_sample)
    return (y_prompt, y_sample)
```

```python
import contextlib
import os
import numpy as np
import concourse.bass as bass
import concourse.mybir as mybir
from concourse.ap import AP
from concourse.bass_utils import run_bass_kernel_spmd

F32 = mybir.dt.float32
BF16 = mybir.dt.bfloat16
ACT = mybir.ActivationFunctionType
ALU = mybir.AluOpType
AX = mybir.AxisListType

D = 2048
DIN = 6656
HD = 128
NEG = -30000.0
EPS = 1e-6
NCORES = 8
C_QA, C_KA, C_VA, C_ZA, C_QB, C_KB, C_VB, C_ZB = 0, 1024, 1280, 1536, 2560, 3584, 4608, 5632
SEQS = [(64, "p"), (16, "s")]


class Ctr:
    def __init__(self, nc, es, name):
        self.h = es.enter_context(nc.semaphore(name))
        self.n = 0
        self.nn = 0
        self.map = {}
        self.name = name


class Tk:
    __slots__ = ("w", "r")

    def __init__(self):
        self.w = None
        self.r = []


def build_program():
    _, used = _build(None)
    nc, _ = _build(used)
    return nc


def _build(needed):
    nc = bass.Bass("TRN2", target_bir_lowering=False)
    used = {}
    es = contextlib.ExitStack()

    def dram(name, shape, kind="ExternalInput", dt=F32):
        return nc.dram_tensor(name, list(shape), dt, kind=kind).ap()

    xall = {"p": dram("xall_p", [8192, D]), "s": dram("xall_s", [2048, D])}
    xloc = dram("xloc", [4, 9 * 128, D])
    ropekv = dram("ropekv", [65, 128, 2, 128])
    ropeq = dram("ropeq", [16, 128, 2, 128])
    w_in = dram("w_in", [D, DIN])
    w_out = dram("w_out", [D, D])
    normw_t = dram("normw_t", [128, 16])
    gains = dram("gains", [4, 128])
    identd = dram("identd", [128, 128])
    prevd = dram("prevd", [128, 128])
    rw = dram("rw", [4, 4, 2, 8, 10, 128])
    cmd = dram("cmd", [128, 64])
    yout = dram("y", [4, 512, D], kind="ExternalOutput")
    wbf = dram("wbf", [34, 128, 16 * 256], kind="Internal", dt=BF16)

    sb_total = [0]

    def sb(name, shape, dt):
        n = 1
        for d_ in shape[1:]:
            n *= d_
        sb_total[0] += n * (4 if dt == F32 else 2)
        return es.enter_context(nc.sbuf_tensor(name, list(shape), dt))

    def ps(name, shape, dt):
        return es.enter_context(nc.psum_tensor(name, list(shape), dt))

    with es:
        PE, ACTE, DVE, POOL, SP = nc.tensor, nc.scalar, nc.vector, nc.gpsimd, nc.sync
        ctr = {n: Ctr(nc, es, n) for n in ("pe", "act", "dve", "pool")}
        eng_of = {"pe": PE, "act": ACTE, "dve": DVE, "pool": POOL, "sp": SP}
        waited = {}

        def wait(engn, ev):
            if ev is None:
                return
            c, v = ev
            key = (engn, c.name)
            if waited.get(key, 0) >= v:
                return
            waited[key] = v
            if c.name in ctr:
                used.setdefault(c.name, set()).add(v)
                eng_of[engn].wait_ge(c.h, c.map[v])
            else:
                eng_of[engn].wait_ge(c.h, v)

        def op(engn, fn, reads=(), writes=(), inc=True, dma=None):
            for t in reads:
                wait(engn, t.w)
            for t in writes:
                wait(engn, t.w)
                for e in t.r:
                    wait(engn, e)
            ins = fn()
            if dma is not None:
                ins.then_inc(dma.h, 16)
                dma.n += 16
                ev = (dma, dma.n)
            elif inc:
                c = ctr[engn]
                c.n += 1
                if needed is None or c.n in needed.get(c.name, ()):
                    ins.then_inc(c.h, 1)
                    c.nn += 1
                c.map[c.n] = c.nn
                ev = (c, c.n)
            else:
                return None
            mark(ev, reads, writes)
            return ev

        def mark(ev, reads=(), writes=()):
            for t in reads:
                t.r = [e for e in t.r if e[0] is not ev[0]] + [ev]
            for t in writes:
                t.w = ev
                t.r = []

        def pe_ev():
            return (ctr["pe"], ctr["pe"].n)

        def dmasem(name):
            return Ctr(nc, es, name)

        identb = sb("identb", [128, 128], BF16)
        prevb = sb("prevb", [128, 128], BF16)
        normw = sb("normw", [128, 16], F32)
        gt = sb("gt", [128, 4, 128], F32); t_gt = Tk()
        gmx = sb("gmx", [128, 4], F32)
        negc = sb("negc", [128, 2], F32); t_negc = Tk()
        epst = sb("epst", [128, 1], F32)
        onest = sb("onest", [128, 1], F32)
        cmb = sb("cmb", [128, 64], BF16); t_cmb = Tk()
        NWS = 2
        ws = [sb(f"ws{i}", [128, 16, 256], BF16) for i in range(NWS)]; t_ws = [Tk() for _ in range(NWS)]
        xf = [sb(f"xf{i}", [128, D], F32) for i in range(2)]; t_xf = [Tk(), Tk()]
        xb = [sb(f"xb{i}", [128, D], BF16) for i in range(2)]; t_xb = [Tk(), Tk()]
        st = [sb(f"st{i}", [128, 4], F32) for i in range(4)]; t_st = [Tk() for _ in range(4)]
        rloc = sb("rloc", [128, 9, 4], F32); t_rloc = [Tk() for _ in range(9)]
        xnTl = sb("xnTl", [128, 16, 9 * 128], BF16); t_xnTl = [(Tk(), Tk()) for _ in range(9)]
        OT = sb("OT", [128, 16, 512], BF16); t_OT = Tk()
        KAT = sb("KAT", [128, 2, 8208], BF16); t_KAT = Tk()
        VAE = sb("VAE", [128, 65, 2, 130], BF16); t_VAE = Tk()
        rope_t = [sb(f"rope{i}", [128, 2, 128], F32) for i in range(2)]; t_rope = [Tk(), Tk()]
        wk = [sb(f"wk{i}", [128, 256], F32) for i in range(2)]; t_wk = [Tk(), Tk()]
        wxg = [sb(f"wxg{i}", [128, 256], F32) for i in range(2)]; t_wxg = [Tk(), Tk()]
        wt1 = [sb(f"wt1{i}", [128, 256], F32) for i in range(2)]; t_wt1 = [Tk(), Tk()]
        wt2 = [sb(f"wt2{i}", [128, 256], F32) for i in range(2)]; t_wt2 = [Tk(), Tk()]
        wst = [sb(f"wst{i}", [128, 8], F32) for i in range(2)]; t_wst = [Tk(), Tk()]
        qtok = [sb(f"qtok{i}", [128, 256], BF16) for i in range(2)]; t_qtok = [Tk(), Tk()]
        QT = sb("QT", [128, 2, 512], BF16); t_QT = Tk()
        KBT = sb("KBT", [128, 2, 9 * 128], BF16); t_KBT = Tk()
        VBE = sb("VBE", [128, 9, 2, 130], BF16); t_VBE = Tk()
        ZG = sb("ZG", [128, 4, 256], BF16); t_ZG = Tk()
        QTAs = [sb(f"QTA{i}", [128, 2, 512], BF16) for i in range(2)]; t_QTAs = [Tk(), Tk()]
        ZGAs = [sb(f"ZGA{i}", [128, 4, 256], BF16) for i in range(2)]; t_ZGAs = [Tk(), Tk()]
        BT = [sb(f"BT{i}", [128, 2, 640], BF16) for i in range(4)]; t_BT = [Tk() for _ in range(4)]
        PT = [sb(f"PT{i}", [128, 1024], BF16) for i in range(2)]; t_PT = [Tk(), Tk()]
        rl = [sb(f"rl{i}", [128, 1], F32) for i in range(2)]; t_rl = [Tk(), Tk()]
        og = [sb(f"og{i}", [128, 128], BF16) for i in range(2)]; t_og = [Tk(), Tk()]
        xres, t_xres = wxg, t_wxg
        ysb, t_ysb = wt1, t_wt1
        if os.environ.get("KDEBUG"):
            print("SBUF bytes/partition allocated:", sb_total[0], "remaining:", nc.sbuf_bytes_remaining)
        psA = ps("psA", [128, 4, 512], F32); t_bk = [Tk() for _ in range(4)]
        ps_m = ps("ps_m", [128, 2, 1024], BF16); t_ps_m = [Tk(), Tk()]
        ps_o = ps("ps_o", [128, 2, 512], F32); t_ps_o = [Tk(), Tk()]

        def pst(hf, alt=0):
            src = psA[:, 2 + hf, :] if alt == 0 else ps_o[:, hf, :]
            return src.bitcast(BF16).rearrange("p (c t) -> p c t", c=8)

        def t_pst(hf, alt=0):
            return t_bk[2 + hf] if alt == 0 else t_ps_o[hf]

        def psu(pu):
            if pu == 2:
                return ps_m[:, 0, :].bitcast(F32)[:, 0:256]
            return psA[:, pu, 0:256]

        def t_pu(pu):
            return t_ps_m[0] if pu == 2 else t_bk[pu]

        def pso(j):
            return ps_o[:, j // 2, (j % 2) * 256:(j % 2) * 256 + 129]

        d_c = dmasem("d_const"); d_cp = dmasem("d_constp")
        d_xf = [dmasem("d_xf0"), dmasem("d_xf1")]
        d_wsp = [dmasem(f"d_wsp{i}") for i in range(NWS)]
        d_wss = [dmasem(f"d_wss{i}") for i in range(NWS)]
        d_wst = [dmasem(f"d_wst{i}") for i in range(NWS)]
        d_rope = [dmasem("d_rope0"), dmasem("d_rope1")]
        d_bt = [dmasem(f"d_bt{i}") for i in range(4)]
        d_xres = [dmasem("d_xr0"), dmasem("d_xr1")]
        d_out = [dmasem("d_out0"), dmasem("d_out1")]

        dummy = Tk()
        op("pool", lambda: POOL.dma_start(out=identb[:], in_=identd[:, :]), dma=d_cp)
        op("pool", lambda: POOL.dma_start(out=prevb[:], in_=prevd[:, :]), dma=d_cp)
        op("pool", lambda: POOL.dma_start(out=cmb[:], in_=cmd[:, :]), dma=d_cp)
        op("sp", lambda: SP.dma_start(out=normw[:], in_=normw_t[:, :]), dma=d_c)
        for gi in range(4):
            op("sp", lambda gi=gi: SP.dma_start(out=gt[:, gi, :], in_=gains[gi:gi + 1, :].partition_broadcast(128)), dma=d_c)
        for e_ in ("dve", "act", "pe", "pool"):
            wait(e_, (d_c, d_c.n))
            wait(e_, (d_cp, d_cp.n))
        t_c0 = Tk()
        op("dve", lambda: DVE.memset(epst[:], EPS), writes=[t_c0])
        op("dve", lambda: DVE.memset(onest[:], 1.0), writes=[t_c0])
        gsq = xf[0][:, 0:512].rearrange("p (g d) -> p g d", g=4)
        op("dve", lambda: DVE.tensor_tensor(out=gsq, in0=gt[:], in1=gt[:], op=ALU.mult), writes=[t_c0, t_xf[0]])
        op("dve", lambda: DVE.tensor_reduce(out=gmx[:], in_=gsq, axis=AX.X, op=ALU.max), reads=[t_c0, t_xf[0]], writes=[t_c0])
        op("dve", lambda: DVE.tensor_tensor(out=negc[:, 0:1], in0=gmx[:, 0:1], in1=gmx[:, 1:2], op=ALU.mult), reads=[t_c0], writes=[t_negc])
        op("dve", lambda: DVE.tensor_tensor(out=negc[:, 1:2], in0=gmx[:, 2:3], in1=gmx[:, 3:4], op=ALU.mult), reads=[t_c0], writes=[t_negc])
        op("act", lambda: ACTE.activation(out=negc[:], in_=negc[:], func=ACT.Ln, scale=128.0), reads=[t_negc], writes=[t_negc])
        op("act", lambda: ACTE.activation(out=negc[:], in_=negc[:], func=ACT.Exp, scale=0.5), reads=[t_negc], writes=[t_negc])
        op("dve", lambda: DVE.tensor_scalar(out=negc[:], in0=negc[:], scalar1=-1.0, scalar2=None, op0=ALU.mult), reads=[t_negc], writes=[t_negc])
        sc = float(HD) ** -0.5
        op("dve", lambda: DVE.tensor_scalar(out=gt[:, 0, :], in0=gt[:, 0, :], scalar1=sc, scalar2=None, op0=ALU.mult), reads=[t_c0], writes=[t_gt])
        op("dve", lambda: DVE.tensor_scalar(out=gt[:, 2, :], in0=gt[:, 2, :], scalar1=sc, scalar2=None, op0=ALU.mult), reads=[t_c0], writes=[t_gt])
        op("dve", lambda: DVE.memset(VAE[:].rearrange("p a b c -> p (a b c)"), 1.0), writes=[t_VAE])
        op("dve", lambda: DVE.memset(VBE[:].rearrange("p a b c -> p (a b c)"), 1.0), writes=[t_VBE])
        for e_ in ("act", "pe", "pool"):
            wait(e_, (ctr["dve"], ctr["dve"].n))

        state = {"xi": 0, "wi": 0, "ui": 0, "mi": 0, "qi": 0, "ri": 0, "si": 0, "pi": 0, "oi": 0, "bi": 0, "yi": 0, "ki": 0, "ti": 0}
        cached = set()
        t_wbf = {}

        def gidx_in(col0):
            return col0 // 256

        deferred_store = [None]

        def load_w(g):
            i = state["wi"] % NWS
            state["wi"] += 1
            if g in cached:
                op("sp", lambda: SP.dma_start(out=ws[i][:].rearrange("p c n -> p (c n)"), in_=wbf[g, :, :]),
                   reads=[t_wbf[g]], writes=[t_ws[i]], dma=d_wss[i])
                return i
            if g < 26:
                src = w_in[:, g * 256:(g + 1) * 256].rearrange("(c p) n -> p c n", p=128)
            else:
                src = w_out[:, (g - 26) * 256:(g - 25) * 256].rearrange("(c p) n -> p c n", p=128)
            op("pool", lambda: POOL.dma_start(out=ws[i][:], in_=src), writes=[t_ws[i]], dma=d_wsp[i])
            if g < 26:
                op("dve", lambda: DVE.tensor_tensor(out=ws[i][:], in0=ws[i][:],
                                                    in1=normw[:, :].unsqueeze(2).broadcast_to([128, 16, 256]), op=ALU.mult),
                   writes=[t_ws[i]])
            t_wbf[g] = Tk()
            deferred_store[0] = (lambda: op(
                "sp", lambda: SP.dma_start(out=wbf[g, :, :], in_=ws[i][:].rearrange("p c n -> p (c n)")),
                reads=[t_ws[i]], writes=[t_wbf[g]], dma=d_wst[i]))
            cached.add(g)
            return i

        class WQ:
            def __init__(self, groups):
                self.groups = list(groups)
                self.pos = 0
                self.slots = {}
                self.pending = []
                self._issue()

            def _issue(self):
                if self.pos < len(self.groups):
                    deferred_store[0] = None
                    self.slots[self.pos] = load_w(self.groups[self.pos])
                    if deferred_store[0] is not None:
                        self.pending.append((self.pos, deferred_store[0]))
                    self.pos += 1

            def flush(self, below=None):
                keep = []
                for qi, fn in self.pending:
                    if below is None or qi < below:
                        fn()
                    else:
                        keep.append((qi, fn))
                self.pending = keep

            def get(self, k):
                self.flush(below=k)
                while self.pos <= k + NWS - 1 and self.pos < len(self.groups):
                    self._issue()
                return self.slots[k]

        def gen_pipe(items, stages):
            n = len(items)
            maxd = max(d for d, _ in stages)
            for s_ in range(n + maxd):
                for d_, fn in stages:
                    i = s_ - d_
                    if 0 <= i < n:
                        fn(items[i])
                yield

        def run_pipe(items, stages):
            for _ in gen_pipe(items, stages):
                pass

        def nt_load(item):
            item["xi"] = state["xi"] % 2
            state["xi"] += 1
            i = item["xi"]
            op("sp", lambda: SP.dma_start(out=xf[i][:], in_=item["src"]), writes=[t_xf[i]], dma=d_xf[i])

        def nt_norm(item):
            i = item["xi"]
            rs, t_rs = item["rs"]
            op("dve", lambda: DVE.tensor_copy(out=xb[i][:], in_=xf[i][:]), reads=[t_xf[i]], writes=[t_xb[i]])
            op("act", lambda: ACTE.activation(out=xf[i][:], in_=xf[i][:], func=ACT.Square, accum_out=rs[:, 0:1]),
               writes=[t_xf[i], t_rs])
            op("act", lambda: ACTE.activation(out=rs[:, 1:2], in_=rs[:, 0:1], func=ACT.Ln, scale=1.0 / D, bias=epst[:]),
               reads=[t_rs], writes=[t_rs])
            op("act", lambda: ACTE.activation(out=rs[:, 2:3], in_=rs[:, 1:2], func=ACT.Exp, scale=-0.5),
               reads=[t_rs], writes=[t_rs])
            op("act", lambda: ACTE.mul(out=rs[:, 3:4], in_=rs[:, 2:3], mul=-1.0), reads=[t_rs], writes=[t_rs])

        def nt_tr(item):
            i = item["xi"]
            dst = item["dst"]
            alt = state["ti"] % 2
            state["ti"] += 1
            for hf in range(2):
                tk_ = t_pst(hf, alt)
                for c8 in range(8):
                    c = hf * 8 + c8
                    op("pe", lambda c=c, hf=hf, c8=c8: PE.transpose(out=pst(hf, alt)[:, c8, :], in_=xb[i][:, c * 128:(c + 1) * 128],
                                                                   identity=identb[:]),
                       reads=[t_xb[i]] if c8 == 0 else [], writes=[tk_] if c8 == 0 else [], inc=(c8 == 7))
                mark(pe_ev(), reads=[t_xb[i]], writes=[tk_])
                if hf == 0:
                    op("dve", lambda: DVE.tensor_copy(out=dst[:, 0:8, :], in_=pst(0, alt)), reads=[tk_], writes=[item["t_dst"][0]])
                else:
                    op("dve", lambda: DVE.tensor_copy(out=dst[:, 8:16, :], in_=pst(1, alt)), reads=[tk_], writes=[item["t_dst"][1]])

        def proj(xT_fn, t_x, wslot, pu):
            txl = list(t_x) if isinstance(t_x, (list, tuple)) else [t_x]
            for c in range(16):
                op("pe", lambda c=c: PE.matmul(psu(pu), lhsT=xT_fn(c), rhs=ws[wslot][:, c, :], start=(c == 0), stop=(c == 15)),
                   reads=txl + [t_ws[wslot]] if c == 0 else [], writes=[t_pu(pu)] if c == 0 else [], inc=(c == 15))
            mark(pe_ev(), reads=txl + [t_ws[wslot]], writes=[t_pu(pu)])

        def next_pu():
            if state.get("pu_mode") == "single":
                return 2
            pu = state["ui"] % 2
            state["ui"] += 1
            return pu

        def qk_evac(item, pu):
            k = state["ki"] % 2
            state["ki"] += 1
            item["k"] = k
            rs, t_rs = item["rs"]
            op("dve", lambda: DVE.tensor_scalar(out=wk[k][:], in0=psu(pu), scalar1=rs[:, 2:3], scalar2=None, op0=ALU.mult),
               reads=[t_pu(pu), t_rs], writes=[t_wk[k]])

        def qk_stats(item):
            k = item["k"]
            gidx = item["gidx"]
            for h in range(2):
                op("act", lambda h=h: ACTE.activation(out=wt2[k][:, h * 128:(h + 1) * 128], in_=wk[k][:, h * 128:(h + 1) * 128],
                                                      func=ACT.Square, accum_out=wst[k][:, h:h + 1]),
                   reads=[t_wk[k]], writes=[t_wt2[k], t_wst[k]])
            op("act", lambda: ACTE.activation(out=wst[k][:, 2:4], in_=wst[k][:, 0:2], func=ACT.Ln, scale=1.0 / HD, bias=epst[:]),
               reads=[t_wst[k]], writes=[t_wst[k]])
            op("act", lambda: ACTE.activation(out=wst[k][:, 4:6], in_=wst[k][:, 2:4], func=ACT.Exp, scale=-0.5),
               reads=[t_wst[k]], writes=[t_wst[k]])
            gb = gt[:, gidx, :].unsqueeze(1).broadcast_to([128, 2, 128])
            v3 = lambda t_: t_[:].rearrange("p (h d) -> p h d", h=2)
            op("pool", lambda: POOL.tensor_tensor(out=v3(wxg[k]), in0=v3(wk[k]), in1=gb, op=ALU.mult),
               reads=[t_wk[k], t_gt], writes=[t_wxg[k]])
            item["cur"] = (wxg[k], t_wxg[k])
            r = item.get("rope")
            if r is not None:
                rt = rope_t[r]
                Cb = rt[:, 0, :].unsqueeze(1).broadcast_to([128, 2, 128])
                op("pool", lambda: POOL.tensor_tensor(out=v3(wt1[k]), in0=v3(wxg[k]), in1=Cb, op=ALU.mult),
                   reads=[t_wxg[k], t_rope[r]], writes=[t_wt1[k]])
                xv = wxg[k][:].rearrange("p (a s j) -> p a s j", a=4, s=2)
                ov = wt2[k][:].rearrange("p (a s j) -> p a s j", a=4, s=2)
                sv = rt[:, 1, :].rearrange("p (a s j) -> p a s j", a=2, s=2)
                first = True
                for hh in range(2):
                    for s_ in range(2):
                        op("pool", lambda hh=hh, s_=s_: POOL.tensor_tensor(
                            out=ov[:, 2 * hh:2 * hh + 2, s_, :], in0=xv[:, 2 * hh:2 * hh + 2, 1 - s_, :],
                            in1=sv[:, :, s_, :], op=ALU.mult),
                           reads=[t_wxg[k], t_rope[r]] if first else [], writes=[t_wt2[k]] if first else [])
                        first = False
                mark((ctr["pool"], ctr["pool"].n), reads=[t_wxg[k], t_rope[r]], writes=[t_wt2[k]])
                op("pool", lambda: POOL.tensor_tensor(out=wt1[k][:], in0=wt1[k][:], in1=wt2[k][:], op=ALU.add),
                   reads=[t_wt1[k], t_wt2[k]], writes=[t_wt1[k]])
                item["cur"] = (wt1[k], t_wt1[k])

        def qk_final(item):
            k = item["k"]
            cur, t_cur = item["cur"]
            q_ = state["qi"] % 2
            state["qi"] += 1
            rb = wst[k][:, 4:6].unsqueeze(2).broadcast_to([128, 2, 128])
            op("dve", lambda: DVE.tensor_tensor(out=qtok[q_][:].rearrange("p (h d) -> p h d", h=2),
                                                in0=cur[:].rearrange("p (h d) -> p h d", h=2), in1=rb, op=ALU.mult),
               reads=[t_cur, t_wst[k]], writes=[t_qtok[q_]])
            m = 1
            for h in range(2):
                op("pe", lambda h=h: PE.transpose(out=ps_m[:, m, h * 128:(h + 1) * 128], in_=qtok[q_][:, h * 128:(h + 1) * 128],
                                                  identity=identb[:]),
                   reads=[t_qtok[q_]] if h == 0 else [], writes=[t_ps_m[m]] if h == 0 else [], inc=(h == 1))
            mark(pe_ev(), reads=[t_qtok[q_]], writes=[t_ps_m[m]])
            ncols = item.get("ncols", 128)
            dst_fn, t_dst = item["dst_fn"], item["t_dst"]
            for h in range(2):
                op("dve", lambda h=h: DVE.tensor_copy(out=dst_fn(h), in_=ps_m[:, m, h * 128:h * 128 + ncols]),
                   reads=[t_ps_m[m]], writes=[t_dst])

        def load_rope(item, src_ap):
            r = state["ri"] % 2
            state["ri"] += 1
            item["rope"] = r
            op("sp", lambda: SP.dma_start(out=rope_t[r][:], in_=src_ap), writes=[t_rope[r]], dma=d_rope[r])

        def fin_dve(item):
            j = item["acc"]
            a = state["oi"] % 2
            state["oi"] += 1
            item["og"] = a
            acc = pso(j)
            op("dve", lambda: DVE.reciprocal(out=rl[a][:], in_=acc[:, 128:129]), reads=[t_ps_o[j // 2]], writes=[t_rl[a]])
            op("dve", lambda: DVE.scalar_tensor_tensor(out=og[a][:], in0=acc[:, 0:128], scalar=rl[a][:, 0:1], in1=item["zsrc"],
                                                       op0=ALU.mult, op1=ALU.mult),
               reads=[t_ps_o[j // 2], t_rl[a], item.get("t_z", t_ZG)], writes=[t_og[a]])

        def fin_tr(item):
            a = item["og"]
            m = 1
            chunk, tokcol = item["och"], item["otok"]
            op("pe", lambda: PE.transpose(out=ps_m[:, m, 0:128], in_=og[a][:], identity=identb[:]),
               reads=[t_og[a]], writes=[t_ps_m[m]])
            op("dve", lambda: DVE.tensor_copy(out=OT[:, chunk, tokcol:tokcol + 128], in_=ps_m[:, m, 0:128]),
               reads=[t_ps_m[m]], writes=[t_OT])

        class _Stop(Exception):
            pass
        STOP = os.environ.get("KSTOP", "")

        def stop(name):
            if STOP == name:
                raise _Stop()

        def main():
            chunk_id = 0
            for (nt, tag) in SEQS:
                nkc = nt + 1
                wq = WQ([gidx_in(C_KA), gidx_in(C_VA)])
                wka, wva = wq.get(0), wq.get(1)
                items = []
                for t in range(nt + 1):
                    meta = (t == nt)
                    items.append({
                        "t": t, "npart": 16 if meta else 128, "ncols": 16 if meta else 128,
                        "src": xloc[0, 8 * 128:9 * 128, :] if meta else xall[tag][t * 128:(t + 1) * 128, :],
                        "dst": xnTl[:, :, (t % 2) * 128:(t % 2 + 1) * 128], "t_dst": t_xnTl[t % 2],
                        "gidx": 1, "t_dstk": t_KAT, "rs": (st[t % 4][:, :], t_st[t % 4]),
                    })
                for it in items:
                    it["dst_fn"] = (lambda h, it=it: KAT[:, h, it["t"] * 128:it["t"] * 128 + it["npart"]])
                    it["t_dst_k"] = t_KAT

                def kv_proj(it):
                    t = it["t"]
                    o = t % 2
                    npart = it["npart"]
                    load_rope(it, ropekv[t if t < nt else 64, :, :, :])
                    pk = next_pu()
                    proj(lambda c: xnTl[:, c, o * 128:(o + 1) * 128], t_xnTl[o], wka, pk)
                    qk_evac(it, pk)
                    pv = next_pu()
                    proj(lambda c: xnTl[:, c, o * 128:(o + 1) * 128], t_xnTl[o], wva, pv)
                    rs, t_rs = it["rs"]
                    op("dve", lambda: DVE.tensor_scalar(out=VAE[:npart, t, :, 0:128],
                                                        in0=psu(pv)[:npart, :].rearrange("p (h d) -> p h d", h=2),
                                                        scalar1=rs[:npart, 2:3], scalar2=None, op0=ALU.mult),
                       reads=[t_bk[pv], t_rs], writes=[t_VAE])

                def kv_final(it):
                    it2 = dict(it)
                    it2["t_dst"] = t_KAT
                    it2["k"] = it["k"]
                    it2["cur"] = it["cur"]
                    qk_final(it2)

                run_pipe(items, [(0, nt_load), (1, nt_norm), (2, nt_tr), (3, kv_proj), (4, qk_stats), (5, kv_final)])
                wq.flush()
                stop("kv")
                for q in range(2):
                    ck = chunk_id
                    chunk_id += 1
                    LT_ORDER = [2, 3, 4, 5, 0, 1, 6, 7, 8]
                    nitems = [{"src": xloc[ck, lt * 128:(lt + 1) * 128, :], "dst": xnTl[:, :, lt * 128:(lt + 1) * 128],
                               "t_dst": t_xnTl[lt], "rs": (rloc[:, lt, :], t_rloc[lt])} for lt in LT_ORDER]
                    gnorm = gen_pipe(nitems, [(0, nt_load), (1, nt_norm), (2, nt_tr)])
                    groups = []
                    bgr = lambda hp: [gidx_in(C_QB) + hp, gidx_in(C_KB) + hp, gidx_in(C_VB) + hp, gidx_in(C_ZB) + hp]
                    groups += bgr(0)
                    for hp in range(4):
                        groups += [gidx_in(C_QA) + hp, gidx_in(C_ZA) + hp]
                        if hp < 3:
                            groups += bgr(hp + 1)
                    groups += [26 + n for n in range(8)]
                    wq = WQ(groups)
                    gpos = [0]

                    def nextw():
                        s_ = wq.get(gpos[0])
                        gpos[0] += 1
                        return s_

                    def xl(lt):
                        return lambda c, lt=lt: xnTl[:, c, lt * 128:(lt + 1) * 128]

                    def ip_proj(it):
                        lt = it["lt"]
                        it["rs"] = (rloc[:, lt, :], t_rloc[lt])
                        rs, t_rs = it["rs"]
                        pu = next_pu()
                        if it["kind"] == "qa":
                            load_rope(it, ropeq[ck * 4 + it["j"], :, :, :])
                        proj(xl(lt), t_xnTl[lt], it["w"], pu)
                        kind = it["kind"]
                        if kind in ("qb", "kb", "qa"):
                            qk_evac(it, pu)
                        elif kind == "vb":
                            op("dve", lambda: DVE.tensor_scalar(out=VBE[:, lt, :, 0:128],
                                                                in0=psu(pu).rearrange("p (h d) -> p h d", h=2),
                                                                scalar1=rs[:, 2:3], scalar2=None, op0=ALU.mult),
                               reads=[t_pu(pu), t_rs], writes=[t_VBE])
                        else:
                            zdst, t_zdst = (ZG, t_ZG) if kind == "zb" else (it["zga"], it["t_zga"])
                            k = state["ki"] % 2
                            state["ki"] += 1
                            op("act", lambda: ACTE.activation(out=wk[k][:], in_=psu(pu), func=ACT.Exp, scale=rs[:, 3:4]),
                               reads=[t_pu(pu), t_rs], writes=[t_wk[k]])
                            op("act", lambda: ACTE.activation(out=wk[k][:], in_=wk[k][:], func=ACT.Ln, bias=onest[:]),
                               reads=[t_wk[k]], writes=[t_wk[k]])
                            op("act", lambda: ACTE.activation(out=wk[k][:], in_=wk[k][:], func=ACT.Exp, scale=-1.0),
                               reads=[t_wk[k]], writes=[t_wk[k]])
                            op("dve", lambda: DVE.scalar_tensor_tensor(out=zdst[:, it["j"], :], in0=psu(pu), scalar=rs[:, 2:3],
                                                                       in1=wk[k][:], op0=ALU.mult, op1=ALU.mult),
                               reads=[t_pu(pu), t_wk[k], t_rs], writes=[t_zdst])

                    def ip_stats(it):
                        if it["kind"] in ("qb", "kb", "qa"):
                            qk_stats(it)

                    def ip_final(it):
                        if it["kind"] in ("qb", "kb", "qa"):
                            qk_final(it)

                    def gen_bproj(hp):
                        wq_, wk_, wv_, wz_ = nextw(), None, None, None
                        for b in range(4):
                            bi = b
                            for qa in range(2):
                                base = ((((ck * 4 + b) * 2 + qa) * 8 + hp * 2) * 10) * 128
                                src = AP(rw.tensor, base, [[1, 64], [10 * 128, 2], [128, 10], [1, 64]])
                                dst = BT[bi][qa * 64:(qa + 1) * 64, :, :].rearrange("p h (r c) -> p h r c", r=10)
                                op("pool", lambda src=src, dst=dst: POOL.dma_start(out=dst, in_=src),
                                   writes=[t_BT[bi]] if qa == 0 else [], dma=d_bt[bi])
                            t_BT[bi].w = (d_bt[bi], d_bt[bi].n)
                            op("pool", lambda bi=bi: POOL.tensor_tensor(
                                out=BT[bi][:].rearrange("p h (r c) -> p (h r) c", r=10),
                                in0=BT[bi][:].rearrange("p h (r c) -> p (h r) c", r=10),
                                in1=cmb[:, :].unsqueeze(1).broadcast_to([128, 20, 64]), op=ALU.add),
                               writes=[t_BT[bi]])
                        items = []
                        for j in range(4):
                            items.append({"kind": "qb", "lt": 2 + j, "j": j, "w": wq_, "gidx": 2, "t_dst": t_QT,
                                          "dst_fn": (lambda h, j=j: QT[:, h, j * 128:(j + 1) * 128])})
                        yield from gen_pipe(items, [(3, ip_final), (0, ip_proj), (1, ip_stats)])
                        wk_ = nextw()
                        items = []
                        for lt in LT_ORDER:
                            items.append({"kind": "kb", "lt": lt, "w": wk_, "gidx": 3, "t_dst": t_KBT,
                                          "dst_fn": (lambda h, lt=lt: KBT[:, h, lt * 128:(lt + 1) * 128])})
                        yield from gen_pipe(items, [(3, ip_final), (0, ip_proj), (1, ip_stats)])
                        wv_ = nextw()
                        items = [{"kind": "vb", "lt": lt, "w": wv_} for lt in LT_ORDER]
                        yield from gen_pipe(items, [(0, ip_proj)])
                        wz_ = nextw()
                        items = [{"kind": "zb", "lt": 2 + j, "j": j, "w": wz_} for j in range(4)]
                        yield from gen_pipe(items, [(0, ip_proj)])

                    def battn(hp):
                        items = []
                        for b in range(4):
                            for hl in range(2):
                                items.append({"b": b, "hl": hl, "zsrc": ZG[:, b, hl * 128:(hl + 1) * 128],
                                              "och": 8 + hp * 2 + hl, "otok": b * 128})

                        def b_bt(it):
                            it["bt"] = it["b"]

                        def b_scores(it):
                            b, hl = it["b"], it["hl"]
                            if hl == 1:
                                it["bt"] = items[items.index(it) - 1]["bt"]
                            bi = it["bt"]
                            sp_ = state["si"] % 2
                            state["si"] += 1
                            it["sp"] = sp_
                            B0, B1 = 2 * sp_, 2 * sp_ + 1
                            for kt in range(5):
                                bank = B0 if kt < 4 else B1
                                outp = psA[:, bank, (kt % 4) * 128:(kt % 4) * 128 + 128]
                                first = (kt == 0)
                                op("pe", lambda kt=kt, outp=outp: PE.matmul(
                                    outp, lhsT=KBT[:, hl, (b + kt) * 128:(b + kt + 1) * 128], rhs=QT[:, hl, b * 128:(b + 1) * 128],
                                    start=True, stop=False),
                                   reads=[t_KBT, t_QT] if first else [], writes=[t_bk[B0], t_bk[B1]] if first else [], inc=False)
                                op("pe", lambda kt=kt, outp=outp: PE.matmul(
                                    outp, lhsT=BT[bi][:, hl, kt * 128:(kt + 1) * 128], rhs=prevb[:], start=False, stop=True),
                                   reads=[t_BT[bi]] if first else [], inc=False)
                            op("pe", lambda: PE.matmul(psA[0:16, B1, 128:256], lhsT=KBT[:, hl, 8 * 128:8 * 128 + 16],
                                                       rhs=QT[:, hl, b * 128:(b + 1) * 128], start=True, stop=True))
                            mark(pe_ev(), reads=[t_KBT, t_QT, t_BT[bi]], writes=[t_bk[B0], t_bk[B1]])
                            p_ = state["pi"] % 2
                            state["pi"] += 1
                            it["pt"] = p_
                            op("act", lambda: ACTE.activation(out=PT[p_][:, 0:512], in_=psA[:, B0, :], func=ACT.Exp, bias=negc[:, 1:2]),
                               reads=[t_bk[B0], t_negc], writes=[t_PT[p_]])
                            op("act", lambda: ACTE.activation(out=PT[p_][:, 512:640], in_=psA[:, B1, 0:128], func=ACT.Exp, bias=negc[:, 1:2]),
                               reads=[t_bk[B1]], writes=[t_PT[p_]])
                            op("act", lambda: ACTE.activation(out=PT[p_][0:16, 640:768], in_=psA[0:16, B1, 128:256], func=ACT.Exp,
                                                              bias=negc[0:16, 1:2]),
                               reads=[t_bk[B1]], writes=[t_PT[p_]])

                        def b_pv(it):
                            b, hl, p_ = it["b"], it["hl"], it["pt"]
                            j = (state["oi2"] % 2) * 2 if "oi2" in state else 0
                            state["oi2"] = state.get("oi2", 0) + 1
                            it["acc"] = j
                            for kt in range(5):
                                op("pe", lambda kt=kt: PE.matmul(
                                    pso(j), lhsT=PT[p_][:, kt * 128:(kt + 1) * 128], rhs=VBE[:, b + kt, hl, 0:129],
                                    start=(kt == 0), stop=False),
                                   reads=[t_PT[p_], t_VBE] if kt == 0 else [], writes=[t_ps_o[j // 2]] if kt == 0 else [], inc=False)
                            op("pe", lambda: PE.matmul(pso(j), lhsT=PT[p_][0:16, 640:768], rhs=VBE[0:16, 8, hl, 0:129],
                                                       start=False, stop=True))
                            mark(pe_ev(), reads=[t_PT[p_], t_VBE], writes=[t_ps_o[j // 2]])
                            fin_dve(it)

                        n_it = len(items)
                        for s_ in range(-2, n_it + 2):
                            if 0 <= s_ + 2 < n_it:
                                b_bt(items[s_ + 2])
                            if 0 <= s_ < n_it:
                                b_scores(items[s_])
                            if 0 <= s_ - 1 < n_it:
                                b_pv(items[s_ - 1])
                            if 0 <= s_ - 2 < n_it:
                                fin_tr(items[s_ - 2])

                    def gen_aproj(hp):
                        QTA, t_QTA, ZGA, t_ZGA = QTAs[hp % 2], t_QTAs[hp % 2], ZGAs[hp % 2], t_ZGAs[hp % 2]
                        wq_ = nextw()
                        items = []
                        for j in range(4):
                            items.append({"kind": "qa", "lt": 2 + j, "j": j, "w": wq_, "gidx": 0, "t_dst": t_QTA,
                                          "dst_fn": (lambda h, j=j: QTA[:, h, j * 128:(j + 1) * 128])})
                        yield from gen_pipe(items, [(3, ip_final), (0, ip_proj), (1, ip_stats)])
                        wz_ = nextw()
                        items = [{"kind": "za", "lt": 2 + j, "j": j, "w": wz_, "zga": ZGA, "t_zga": t_ZGA} for j in range(4)]
                        yield from gen_pipe(items, [(0, ip_proj)])

                    def gen_next(hp):
                        yield from gen_bproj(hp)
                        yield from gen_aproj(hp)

                    def aphase(hp, genB, nsteps):
                        g = hp // 2
                        QTA, t_QTA, ZGA, t_ZGA = QTAs[hp % 2], t_QTAs[hp % 2], ZGAs[hp % 2], t_ZGAs[hp % 2]
                        if genB is not None:
                            state["pu_mode"] = "single"
                        ucount = [0]
                        done_steps = [0]
                        nunits_tot = 2 * (nt // 2 + 1)
                        for hl in range(2):
                            units = [[2 * p, 2 * p + 1] for p in range(nt // 2)] + [[nt]]

                            def S(u):
                                cs = units[u]
                                sp_ = u % 2
                                p_ = state["pi"] % 2
                                state["pi"] += 1
                                if len(cs) == 2:
                                    for i_, c in enumerate(cs):
                                        bank = 2 * sp_ + i_
                                        op("pe", lambda c=c, bank=bank: PE.matmul(psA[:, bank, :], lhsT=KAT[:, g, c * 128:(c + 1) * 128],
                                                                                  rhs=QTA[:, hl, :], start=True, stop=True),
                                           reads=[t_KAT, t_QTA] if i_ == 0 else [],
                                           writes=[t_bk[2 * sp_], t_bk[2 * sp_ + 1]] if i_ == 0 else [], inc=(i_ == 1))
                                    mark(pe_ev(), reads=[t_KAT, t_QTA], writes=[t_bk[2 * sp_], t_bk[2 * sp_ + 1]])
                                    op("act", lambda: ACTE.activation(out=PT[p_][:].rearrange("p (a n) -> p a n", a=2),
                                                                      in_=psA[:, 2 * sp_:2 * sp_ + 2, :], func=ACT.Exp, bias=negc[:, 0:1]),
                                       reads=[t_bk[2 * sp_], t_bk[2 * sp_ + 1], t_negc], writes=[t_PT[p_]])
                                else:
                                    c = cs[0]
                                    bank = 2 * sp_
                                    op("pe", lambda: PE.matmul(psA[:16, bank, :], lhsT=KAT[:, g, c * 128:c * 128 + 16], rhs=QTA[:, hl, :],
                                                               start=True, stop=True),
                                       reads=[t_KAT, t_QTA], writes=[t_bk[bank]])
                                    op("act", lambda: ACTE.activation(out=PT[p_][:16, 0:512], in_=psA[:16, bank, :], func=ACT.Exp,
                                                                      bias=negc[:16, 0:1]),
                                       reads=[t_bk[bank], t_negc], writes=[t_PT[p_]])
                                return p_

                            def PV(u, p_):
                                cs = units[u]
                                last_u = (u == len(units) - 1)
                                for i_, c in enumerate(cs):
                                    kc = 128 if c < nt else 16
                                    for j in range(4):
                                        firstmm = (u == 0 and i_ == 0)
                                        lastmm = (last_u and i_ == len(cs) - 1)
                                        op("pe", lambda j=j, c=c, i_=i_, kc=kc: PE.matmul(
                                            pso(j), lhsT=PT[p_][:kc, i_ * 512 + j * 128:i_ * 512 + (j + 1) * 128],
                                            rhs=VAE[:kc, c, g, 0:129], start=(firstmm and j % 2 == 0), stop=lastmm,
                                            skip_group_check=True),
                                           reads=[t_PT[p_], t_VAE] if (i_ == 0 and j == 0) else [],
                                           writes=[t_ps_o[0], t_ps_o[1]] if (firstmm and j == 0) else [],
                                           inc=(i_ == len(cs) - 1 and j == 3))
                                mark(pe_ev(), reads=[t_PT[p_], t_VAE], writes=[t_ps_o[0], t_ps_o[1]] if last_u else [])

                            pend = S(0)
                            for u in range(len(units)):
                                nxt = S(u + 1) if u + 1 < len(units) else None
                                PV(u, pend)
                                pend = nxt
                                ucount[0] += 1
                                if genB is not None:
                                    want = min((ucount[0] * nsteps) // nunits_tot, done_steps[0] + 1)
                                    while done_steps[0] < want:
                                        next(genB, None)
                                        done_steps[0] += 1
                            fitems = [{"acc": j, "zsrc": ZGA[:, j, hl * 128:(hl + 1) * 128], "t_z": t_ZGA,
                                       "och": hp * 2 + hl, "otok": j * 128}
                                      for j in range(4)]
                            run_pipe(fitems, [(0, fin_dve), (1, fin_tr)])

                    g0 = gen_next(0)
                    step_ = 0
                    for _ in gnorm:
                        if step_ >= 4:
                            next(g0, None)
                        step_ += 1
                    for _ in g0:
                        pass
                    for hp in range(4):
                        battn(hp)
                        genB = gen_next(hp + 1) if hp < 3 else None
                        aphase(hp, genB, 46)
                        state["pu_mode"] = "double"
                        if genB is not None:
                            for _ in genB:
                                pass
                    stop("A")
                    for n in range(8):
                        wo = nextw()
                        for j in range(4):
                            a = state["yi"] % 2
                            state["yi"] += 1
                            op("sp", lambda: SP.dma_start(out=xres[a][:], in_=xloc[ck, (2 + j) * 128:(3 + j) * 128, n * 256:(n + 1) * 256]),
                               writes=[t_xres[a]], dma=d_xres[a])
                            pu = next_pu()
                            proj(lambda c, j=j: OT[:, c, j * 128:(j + 1) * 128], t_OT, wo, pu)
                            op("dve", lambda: DVE.tensor_tensor(out=ysb[a][:], in0=psu(pu), in1=xres[a][:], op=ALU.add),
                               reads=[t_bk[pu], t_xres[a]], writes=[t_ysb[a]])
                            op("act", lambda: ACTE.dma_start(out=yout[ck, j * 128:(j + 1) * 128, n * 256:(n + 1) * 256], in_=ysb[a][:]),
                               reads=[t_ysb[a]], dma=d_out[a])
                    wq.flush()
        try:
            main()
        except _Stop:
            pass
        SP.wait_ge(d_out[0].h, d_out[0].n)
        SP.wait_ge(d_out[1].h, d_out[1].n)
    return nc, used


def _rope_tables(rows, cols):
    half = 64
    inv = (10000.0 ** (-np.arange(0, half, 2, dtype=np.float32) / half)).astype(np.float32)
    ar = rows.astype(np.float32)[:, None] * inv[None, :]
    ac = cols.astype(np.float32)[:, None] * inv[None, :]
    cr, sr, cc, sc = np.cos(ar), np.sin(ar), np.cos(ac), np.sin(ac)
    C = np.concatenate([cr, cr, cc, cc], axis=1)
    S = np.concatenate([-sr, sr, -sc, sc], axis=1)
    return np.stack([C, S], axis=1).astype(np.float32)


def _tile_rope(T):
    p = np.arange(128)
    return _rope_tables(2 * T + p // 64, p % 64)


def _meta_rope():
    r = np.zeros((128,), np.int64) - 1
    c = np.arange(128) % 16
    return _rope_tables(r, c)


_NC_CACHE = {}


def kernel(x_prompt, x_sample, meta_tokens, norm_w, w_in, q_norm_a, k_norm_a, q_norm_b, k_norm_b, rpb, w_out):
    f32 = np.float32
    x_prompt = np.asarray(x_prompt, f32); x_sample = np.asarray(x_sample, f32)
    meta_tokens = np.asarray(meta_tokens, f32)
    w_in2 = np.ascontiguousarray(np.asarray(w_in, f32)[0]); w_out2 = np.ascontiguousarray(np.asarray(w_out, f32)[0])
    rpb2 = np.asarray(rpb, f32)[0]
    normw_t = np.ascontiguousarray(np.asarray(norm_w, f32)[0].reshape(16, 128).T)
    gains = np.stack([np.asarray(q_norm_a, f32)[0], np.asarray(k_norm_a, f32)[0],
                      np.asarray(q_norm_b, f32)[0], np.asarray(k_norm_b, f32)[0]]).astype(f32)
    ident = np.eye(128, dtype=f32)
    prev = np.zeros((128, 128), f32)
    for qa in range(2):
        for c in range(64):
            prev[qa * 64 + c, qa * 64 + 63 - c] = 1.0
    cm = np.full((128, 64), NEG, f32)
    for qa in range(2):
        for qcc in range(64):
            qc = 63 - qcc
            c0 = min(max(qc - 8, 0), 48)
            cm[qa * 64 + qcc, c0:c0 + 16] = 0.0
    ropekv = np.stack([_tile_rope(T) for T in range(64)] + [_meta_rope()]).astype(f32)
    meta_tile = np.zeros((128, D), f32); meta_tile[:16] = meta_tokens

    in_maps = []
    for i in range(NCORES):
        si, hi = i // 2, i % 2
        chunks = [("p", x_prompt[0], 64, 8 * i), ("p", x_prompt[0], 64, 8 * i + 4),
                  ("s", x_sample[si], 16, 8 * hi), ("s", x_sample[si], 16, 8 * hi + 4)]
        xloc = np.zeros((4, 9 * 128, D), f32)
        rwa = np.full((4, 4, 2, 8, 10, 128), NEG, f32)
        ropeq = np.zeros((16, 128, 2, 128), f32)
        for ck, (tag, xs, nb, T0) in enumerate(chunks):
            rows = 2 * nb
            gt_of = [None] * 8
            is_copy = [False] * 8
            for L in range(8):
                G = T0 - 2 + L
                if 0 <= G < nb:
                    gt_of[L] = G
            if T0 == 0:
                gt_of[1] = 3; is_copy[1] = True
            if T0 + 4 == nb:
                gt_of[6] = nb - 4; is_copy[6] = True
            for L in range(8):
                if gt_of[L] is not None:
                    xloc[ck, L * 128:(L + 1) * 128] = xs[gt_of[L] * 128:(gt_of[L] + 1) * 128]
            xloc[ck, 8 * 128:9 * 128] = meta_tile
            for b in range(4):
                Tb = T0 + b
                ropeq[ck * 4 + b] = ropekv[Tb]
                slots = list(range(b, b + 5))
                originals = {gt_of[L] for L in slots if gt_of[L] is not None and not is_copy[L]}
                for qa in range(2):
                    qr = 2 * Tb + qa
                    r0 = min(max(qr - 4, 0), rows - 8)
                    for kr_rel in range(10):
                        L = b + kr_rel // 2
                        G = gt_of[L]
                        if G is None:
                            continue
                        if is_copy[L] and G in originals:
                            continue
                        kr = 2 * G + kr_rel % 2
                        if r0 <= kr < r0 + 8:
                            dr = kr - qr + 7
                            rwa[ck, b, qa, :, kr_rel, 48:79] = rpb2[:, dr, :]
        in_maps.append({
            "xall_p": x_prompt[0], "xall_s": x_sample[si], "xloc": xloc, "ropekv": ropekv, "ropeq": ropeq,
            "w_in": w_in2, "w_out": w_out2, "normw_t": normw_t, "gains": gains, "identd": ident, "prevd": prev,
            "rw": rwa, "cmd": cm,
        })
    if "nc" not in _NC_CACHE:
        _NC_CACHE["nc"] = build_program()
    nc = _NC_CACHE["nc"]
    res = run_bass_kernel_spmd(nc, in_maps, core_ids=list(range(NCORES)))
    y_prompt = np.zeros((1, 8192, D), f32)
    y_sample = np.zeros((4, 2048, D), f32)
    for i in range(NCORES):
        y = np.asarray(res.results[i]["y"], f32)
        si, hi = i // 2, i % 2
        y_prompt[0, 1024 * i:1024 * i + 512] = y[0]
        y_prompt[0, 1024 * i + 512:1024 * (i + 1)] = y[1]
        y_sample[si, 1024 * hi:1024 * hi + 512] = y[2]
        y_sample[si, 1024 * hi + 512:1024 * (hi + 1)] = y[3]
    return (y_prompt, y_sample)
```

```python
import contextlib
import os
import numpy as np
import concourse.bass as bass
import concourse.mybir as mybir
from concourse.ap import AP
from concourse.bass_utils import run_bass_kernel_spmd

F32 = mybir.dt.float32
BF16 = mybir.dt.bfloat16
ACT = mybir.ActivationFunctionType
ALU = mybir.AluOpType
AX = mybir.AxisListType

D = 2048
DIN = 6656
HD = 128
NEG = -30000.0
EPS = 1e-6
NCORES = 8
C_QA, C_KA, C_VA, C_ZA, C_QB, C_KB, C_VB, C_ZB = 0, 1024, 1280, 1536, 2560, 3584, 4608, 5632
SEQS = [(64, "p"), (16, "s")]


class Ctr:
    def __init__(self, nc, es, name):
        self.h = es.enter_context(nc.semaphore(name))
        self.n = 0
        self.nn = 0
        self.map = {}
        self.name = name


class Tk:
    __slots__ = ("w", "r")

    def __init__(self):
        self.w = None
        self.r = []


def build_program():
    _, used = _build(None)
    nc, _ = _build(used)
    return nc


def _build(needed):
    nc = bass.Bass("TRN2", target_bir_lowering=False)
    used = {}
    es = contextlib.ExitStack()

    def dram(name, shape, kind="ExternalInput", dt=F32):
        return nc.dram_tensor(name, list(shape), dt, kind=kind).ap()

    xall = {"p": dram("xall_p", [8192, D]), "s": dram("xall_s", [2048, D])}
    xloc = dram("xloc", [4, 9 * 128, D])
    ropekv = dram("ropekv", [65, 128, 2, 128])
    ropeq = dram("ropeq", [16, 128, 2, 128])
    w_in = dram("w_in", [D, DIN])
    w_out = dram("w_out", [D, D])
    normw_t = dram("normw_t", [128, 16])
    gains = dram("gains", [4, 128])
    identd = dram("identd", [128, 128])
    prevd = dram("prevd", [128, 128])
    rw = dram("rw", [4, 4, 2, 8, 10, 128])
    cmd = dram("cmd", [128, 64])
    yout = dram("y", [4, 512, D], kind="ExternalOutput")
    wbf = dram("wbf", [34, 128, 16 * 256], kind="Internal", dt=BF16)

    sb_total = [0]

    def sb(name, shape, dt):
        n = 1
        for d_ in shape[1:]:
            n *= d_
        sb_total[0] += n * (4 if dt == F32 else 2)
        return es.enter_context(nc.sbuf_tensor(name, list(shape), dt))

    def ps(name, shape, dt):
        return es.enter_context(nc.psum_tensor(name, list(shape), dt))

    with es:
        PE, ACTE, DVE, POOL, SP = nc.tensor, nc.scalar, nc.vector, nc.gpsimd, nc.sync
        ctr = {n: Ctr(nc, es, n) for n in ("pe", "act", "dve", "pool")}
        eng_of = {"pe": PE, "act": ACTE, "dve": DVE, "pool": POOL, "sp": SP}
        waited = {}

        def wait(engn, ev):
            if ev is None:
                return
            c, v = ev
            key = (engn, c.name)
            if waited.get(key, 0) >= v:
                return
            waited[key] = v
            if c.name in ctr:
                used.setdefault(c.name, set()).add(v)
                eng_of[engn].wait_ge(c.h, c.map[v])
            else:
                eng_of[engn].wait_ge(c.h, v)

        def op(engn, fn, reads=(), writes=(), inc=True, dma=None):
            for t in reads:
                wait(engn, t.w)
            for t in writes:
                wait(engn, t.w)
                for e in t.r:
                    wait(engn, e)
            ins = fn()
            if dma is not None:
                ins.then_inc(dma.h, 16)
                dma.n += 16
                ev = (dma, dma.n)
            elif inc:
                c = ctr[engn]
                c.n += 1
                if needed is None or c.n in needed.get(c.name, ()):
                    ins.then_inc(c.h, 1)
                    c.nn += 1
                c.map[c.n] = c.nn
                ev = (c, c.n)
            else:
                return None
            mark(ev, reads, writes)
            return ev

        def mark(ev, reads=(), writes=()):
            for t in reads:
                t.r = [e for e in t.r if e[0] is not ev[0]] + [ev]
            for t in writes:
                t.w = ev
                t.r = []

        def pe_ev():
            return (ctr["pe"], ctr["pe"].n)

        def dmasem(name):
            return Ctr(nc, es, name)

        identb = sb("identb", [128, 128], BF16)
        prevb = sb("prevb", [128, 128], BF16)
        normw = sb("normw", [128, 16], F32)
        gt = sb("gt", [128, 4, 128], F32); t_gt = Tk()
        gmx = sb("gmx", [128, 4], F32)
        negc = sb("negc", [128, 2], F32); t_negc = Tk()
        epst = sb("epst", [128, 1], F32)
        onest = sb("onest", [128, 1], F32)
        cmb = sb("cmb", [128, 64], BF16); t_cmb = Tk()
        NWS = 2
        ws = [sb(f"ws{i}", [128, 16, 256], BF16) for i in range(NWS)]; t_ws = [Tk() for _ in range(NWS)]
        xf = [sb(f"xf{i}", [128, D], F32) for i in range(2)]; t_xf = [Tk(), Tk()]
        xb = [sb(f"xb{i}", [128, D], BF16) for i in range(2)]; t_xb = [Tk(), Tk()]
        st = [sb(f"st{i}", [128, 4], F32) for i in range(4)]; t_st = [Tk() for _ in range(4)]
        rloc = sb("rloc", [128, 9, 4], F32); t_rloc = [Tk() for _ in range(9)]
        xnTl = sb("xnTl", [128, 16, 9 * 128], BF16); t_xnTl = [(Tk(), Tk()) for _ in range(9)]
        OT = sb("OT", [128, 16, 512], BF16); t_OT = Tk()
        KAT = sb("KAT", [128, 2, 8208], BF16); t_KAT = Tk()
        VAE = sb("VAE", [128, 65, 2, 130], BF16); t_VAE = Tk()
        rope_t = [sb(f"rope{i}", [128, 2, 128], F32) for i in range(2)]; t_rope = [Tk(), Tk()]
        wk = [sb(f"wk{i}", [128, 256], F32) for i in range(2)]; t_wk = [Tk(), Tk()]
        wxg = [sb(f"wxg{i}", [128, 256], F32) for i in range(2)]; t_wxg = [Tk(), Tk()]
        wt1 = [sb(f"wt1{i}", [128, 256], F32) for i in range(2)]; t_wt1 = [Tk(), Tk()]
        wt2 = [sb(f"wt2{i}", [128, 256], F32) for i in range(2)]; t_wt2 = [Tk(), Tk()]
        wst = [sb(f"wst{i}", [128, 8], F32) for i in range(2)]; t_wst = [Tk(), Tk()]
        qtok = [sb(f"qtok{i}", [128, 256], BF16) for i in range(2)]; t_qtok = [Tk(), Tk()]
        QT = sb("QT", [128, 2, 512], BF16); t_QT = Tk()
        KBT = sb("KBT", [128, 2, 9 * 128], BF16); t_KBT = Tk()
        VBE = sb("VBE", [128, 9, 2, 130], BF16); t_VBE = Tk()
        ZG = sb("ZG", [128, 4, 256], BF16); t_ZG = Tk()
        QTAs = [sb(f"QTA{i}", [128, 2, 512], BF16) for i in range(2)]; t_QTAs = [Tk(), Tk()]
        ZGAs = [sb(f"ZGA{i}", [128, 4, 256], BF16) for i in range(2)]; t_ZGAs = [Tk(), Tk()]
        BT = [sb(f"BT{i}", [128, 2, 640], BF16) for i in range(4)]; t_BT = [Tk() for _ in range(4)]
        PT = [sb(f"PT{i}", [128, 1024], BF16) for i in range(2)]; t_PT = [Tk(), Tk()]
        rl = [sb(f"rl{i}", [128, 1], F32) for i in range(2)]; t_rl = [Tk(), Tk()]
        og = [sb(f"og{i}", [128, 128], BF16) for i in range(2)]; t_og = [Tk(), Tk()]
        xres, t_xres = wxg, t_wxg
        ysb, t_ysb = wt1, t_wt1
        if os.environ.get("KDEBUG"):
            print("SBUF bytes/partition allocated:", sb_total[0], "remaining:", nc.sbuf_bytes_remaining)
        psA = ps("psA", [128, 4, 512], F32); t_bk = [Tk() for _ in range(4)]
        ps_m = ps("ps_m", [128, 2, 1024], BF16); t_ps_m = [Tk(), Tk()]
        ps_o = ps("ps_o", [128, 2, 512], F32); t_ps_o = [Tk(), Tk()]

        def pst(hf, alt=0):
            src = psA[:, 2 + hf, :] if alt == 0 else ps_o[:, hf, :]
            return src.bitcast(BF16).rearrange("p (c t) -> p c t", c=8)

        def t_pst(hf, alt=0):
            return t_bk[2 + hf] if alt == 0 else t_ps_o[hf]

        def psu(pu):
            if pu == 2:
                return ps_m[:, 0, :].bitcast(F32)[:, 0:256]
            return psA[:, pu, 0:256]

        def t_pu(pu):
            return t_ps_m[0] if pu == 2 else t_bk[pu]

        def pso(j):
            return ps_o[:, j // 2, (j % 2) * 256:(j % 2) * 256 + 129]

        d_c = dmasem("d_const"); d_cp = dmasem("d_constp")
        d_xf = [dmasem("d_xf0"), dmasem("d_xf1")]
        d_wsp = [dmasem(f"d_wsp{i}") for i in range(NWS)]
        d_wss = [dmasem(f"d_wss{i}") for i in range(NWS)]
        d_wst = [dmasem(f"d_wst{i}") for i in range(NWS)]
        d_rope = [dmasem("d_rope0"), dmasem("d_rope1")]
        d_bt = [dmasem(f"d_bt{i}") for i in range(4)]
        d_xres = [dmasem("d_xr0"), dmasem("d_xr1")]
        d_out = [dmasem("d_out0"), dmasem("d_out1")]

        dummy = Tk()
        op("pool", lambda: POOL.dma_start(out=identb[:], in_=identd[:, :]), dma=d_cp)
        op("pool", lambda: POOL.dma_start(out=prevb[:], in_=prevd[:, :]), dma=d_cp)
        op("pool", lambda: POOL.dma_start(out=cmb[:], in_=cmd[:, :]), dma=d_cp)
        op("sp", lambda: SP.dma_start(out=normw[:], in_=normw_t[:, :]), dma=d_c)
        for gi in range(4):
            op("sp", lambda gi=gi: SP.dma_start(out=gt[:, gi, :], in_=gains[gi:gi + 1, :].partition_broadcast(128)), dma=d_c)
        for e_ in ("dve", "act", "pe", "pool"):
            wait(e_, (d_c, d_c.n))
            wait(e_, (d_cp, d_cp.n))
        t_c0 = Tk()
        op("dve", lambda: DVE.memset(epst[:], EPS), writes=[t_c0])
        op("dve", lambda: DVE.memset(onest[:], 1.0), writes=[t_c0])
        gsq = xf[0][:, 0:512].rearrange("p (g d) -> p g d", g=4)
        op("dve", lambda: DVE.tensor_tensor(out=gsq, in0=gt[:], in1=gt[:], op=ALU.mult), writes=[t_c0, t_xf[0]])
        op("dve", lambda: DVE.tensor_reduce(out=gmx[:], in_=gsq, axis=AX.X, op=ALU.max), reads=[t_c0, t_xf[0]], writes=[t_c0])
        op("dve", lambda: DVE.tensor_tensor(out=negc[:, 0:1], in0=gmx[:, 0:1], in1=gmx[:, 1:2], op=ALU.mult), reads=[t_c0], writes=[t_negc])
        op("dve", lambda: DVE.tensor_tensor(out=negc[:, 1:2], in0=gmx[:, 2:3], in1=gmx[:, 3:4], op=ALU.mult), reads=[t_c0], writes=[t_negc])
        op("act", lambda: ACTE.activation(out=negc[:], in_=negc[:], func=ACT.Ln, scale=128.0), reads=[t_negc], writes=[t_negc])
        op("act", lambda: ACTE.activation(out=negc[:], in_=negc[:], func=ACT.Exp, scale=0.5), reads=[t_negc], writes=[t_negc])
        op("dve", lambda: DVE.tensor_scalar(out=negc[:], in0=negc[:], scalar1=-1.0, scalar2=None, op0=ALU.mult), reads=[t_negc], writes=[t_negc])
        sc = float(HD) ** -0.5
        op("dve", lambda: DVE.tensor_scalar(out=gt[:, 0, :], in0=gt[:, 0, :], scalar1=sc, scalar2=None, op0=ALU.mult), reads=[t_c0], writes=[t_gt])
        op("dve", lambda: DVE.tensor_scalar(out=gt[:, 2, :], in0=gt[:, 2, :], scalar1=sc, scalar2=None, op0=ALU.mult), reads=[t_c0], writes=[t_gt])
        op("dve", lambda: DVE.memset(VAE[:].rearrange("p a b c -> p (a b c)"), 1.0), writes=[t_VAE])
        op("dve", lambda: DVE.memset(VBE[:].rearrange("p a b c -> p (a b c)"), 1.0), writes=[t_VBE])
        for e_ in ("act", "pe", "pool"):
            wait(e_, (ctr["dve"], ctr["dve"].n))

        state = {"xi": 0, "wi": 0, "ui": 0, "mi": 0, "qi": 0, "ri": 0, "si": 0, "pi": 0, "oi": 0, "bi": 0, "yi": 0, "ki": 0, "ti": 0}
        cached = set()
        t_wbf = {}

        def gidx_in(col0):
            return col0 // 256

        deferred_store = [None]

        def load_w(g):
            i = state["wi"] % NWS
            state["wi"] += 1
            if g in cached:
                op("sp", lambda: SP.dma_start(out=ws[i][:].rearrange("p c n -> p (c n)"), in_=wbf[g, :, :]),
                   reads=[t_wbf[g]], writes=[t_ws[i]], dma=d_wss[i])
                return i
            if g < 26:
                src = w_in[:, g * 256:(g + 1) * 256].rearrange("(c p) n -> p c n", p=128)
            else:
                src = w_out[:, (g - 26) * 256:(g - 25) * 256].rearrange("(c p) n -> p c n", p=128)
            op("pool", lambda: POOL.dma_start(out=ws[i][:], in_=src), writes=[t_ws[i]], dma=d_wsp[i])
            if g < 26:
                op("dve", lambda: DVE.tensor_tensor(out=ws[i][:], in0=ws[i][:],
                                                    in1=normw[:, :].unsqueeze(2).broadcast_to([128, 16, 256]), op=ALU.mult),
                   writes=[t_ws[i]])
            t_wbf[g] = Tk()
            deferred_store[0] = (lambda: op(
                "sp", lambda: SP.dma_start(out=wbf[g, :, :], in_=ws[i][:].rearrange("p c n -> p (c n)")),
                reads=[t_ws[i]], writes=[t_wbf[g]], dma=d_wst[i]))
            cached.add(g)
            return i

        class WQ:
            def __init__(self, groups):
                self.groups = list(groups)
                self.pos = 0
                self.slots = {}
                self.pending = []
                self._issue()

            def _issue(self):
                if self.pos < len(self.groups):
                    deferred_store[0] = None
                    self.slots[self.pos] = load_w(self.groups[self.pos])
                    if deferred_store[0] is not None:
                        self.pending.append((self.pos, deferred_store[0]))
                    self.pos += 1

            def flush(self, below=None):
                keep = []
                for qi, fn in self.pending:
                    if below is None or qi < below:
                        fn()
                    else:
                        keep.append((qi, fn))
                self.pending = keep

            def get(self, k):
                self.flush(below=k)
                while self.pos <= k + NWS - 1 and self.pos < len(self.groups):
                    self._issue()
                return self.slots[k]

        def gen_pipe(items, stages):
            n = len(items)
            maxd = max(d for d, _ in stages)
            for s_ in range(n + maxd):
                for d_, fn in stages:
                    i = s_ - d_
                    if 0 <= i < n:
                        fn(items[i])
                yield

        def run_pipe(items, stages):
            for _ in gen_pipe(items, stages):
                pass

        def nt_load(item):
            item["xi"] = state["xi"] % 2
            state["xi"] += 1
            i = item["xi"]
            op("sp", lambda: SP.dma_start(out=xf[i][:], in_=item["src"]), writes=[t_xf[i]], dma=d_xf[i])

        def nt_norm(item):
            i = item["xi"]
            rs, t_rs = item["rs"]
            op("dve", lambda: DVE.tensor_copy(out=xb[i][:], in_=xf[i][:]), reads=[t_xf[i]], writes=[t_xb[i]])
            op("act", lambda: ACTE.activation(out=xf[i][:], in_=xf[i][:], func=ACT.Square, accum_out=rs[:, 0:1]),
               writes=[t_xf[i], t_rs])
            op("act", lambda: ACTE.activation(out=rs[:, 1:2], in_=rs[:, 0:1], func=ACT.Ln, scale=1.0 / D, bias=epst[:]),
               reads=[t_rs], writes=[t_rs])
            op("act", lambda: ACTE.activation(out=rs[:, 2:3], in_=rs[:, 1:2], func=ACT.Exp, scale=-0.5),
               reads=[t_rs], writes=[t_rs])
            op("act", lambda: ACTE.mul(out=rs[:, 3:4], in_=rs[:, 2:3], mul=-1.0), reads=[t_rs], writes=[t_rs])

        def nt_tr(item):
            i = item["xi"]
            dst = item["dst"]
            alt = state["ti"] % 2
            state["ti"] += 1
            for hf in range(2):
                tk_ = t_pst(hf, alt)
                for c8 in range(8):
                    c = hf * 8 + c8
                    op("pe", lambda c=c, hf=hf, c8=c8: PE.transpose(out=pst(hf, alt)[:, c8, :], in_=xb[i][:, c * 128:(c + 1) * 128],
                                                                   identity=identb[:]),
                       reads=[t_xb[i]] if c8 == 0 else [], writes=[tk_] if c8 == 0 else [], inc=(c8 == 7))
                mark(pe_ev(), reads=[t_xb[i]], writes=[tk_])
                if hf == 0:
                    op("dve", lambda: DVE.tensor_copy(out=dst[:, 0:8, :], in_=pst(0, alt)), reads=[tk_], writes=[item["t_dst"][0]])
                else:
                    op("dve", lambda: DVE.tensor_copy(out=dst[:, 8:16, :], in_=pst(1, alt)), reads=[tk_], writes=[item["t_dst"][1]])

        def proj(xT_fn, t_x, wslot, pu):
            txl = list(t_x) if isinstance(t_x, (list, tuple)) else [t_x]
            for c in range(16):
                op("pe", lambda c=c: PE.matmul(psu(pu), lhsT=xT_fn(c), rhs=ws[wslot][:, c, :], start=(c == 0), stop=(c == 15)),
                   reads=txl + [t_ws[wslot]] if c == 0 else [], writes=[t_pu(pu)] if c == 0 else [], inc=(c == 15))
            mark(pe_ev(), reads=txl + [t_ws[wslot]], writes=[t_pu(pu)])

        def next_pu():
            if state.get("pu_mode") == "single":
                return 2
            pu = state["ui"] % 2
            state["ui"] += 1
            return pu

        def qk_evac(item, pu):
            k = state["ki"] % 2
            state["ki"] += 1
            item["k"] = k
            rs, t_rs = item["rs"]
            op("dve", lambda: DVE.tensor_scalar(out=wk[k][:], in0=psu(pu), scalar1=rs[:, 2:3], scalar2=None, op0=ALU.mult),
               reads=[t_pu(pu), t_rs], writes=[t_wk[k]])

        def qk_stats(item):
            k = item["k"]
            gidx = item["gidx"]
            for h in range(2):
                op("act", lambda h=h: ACTE.activation(out=wt2[k][:, h * 128:(h + 1) * 128], in_=wk[k][:, h * 128:(h + 1) * 128],
                                                      func=ACT.Square, accum_out=wst[k][:, h:h + 1]),
                   reads=[t_wk[k]], writes=[t_wt2[k], t_wst[k]])
            op("act", lambda: ACTE.activation(out=wst[k][:, 2:4], in_=wst[k][:, 0:2], func=ACT.Ln, scale=1.0 / HD, bias=epst[:]),
               reads=[t_wst[k]], writes=[t_wst[k]])
            op("act", lambda: ACTE.activation(out=wst[k][:, 4:6], in_=wst[k][:, 2:4], func=ACT.Exp, scale=-0.5),
               reads=[t_wst[k]], writes=[t_wst[k]])
            gb = gt[:, gidx, :].unsqueeze(1).broadcast_to([128, 2, 128])
            v3 = lambda t_: t_[:].rearrange("p (h d) -> p h d", h=2)
            op("pool", lambda: POOL.tensor_tensor(out=v3(wxg[k]), in0=v3(wk[k]), in1=gb, op=ALU.mult),
               reads=[t_wk[k], t_gt], writes=[t_wxg[k]])
            item["cur"] = (wxg[k], t_wxg[k])
            r = item.get("rope")
            if r is not None:
                rt = rope_t[r]
                Cb = rt[:, 0, :].unsqueeze(1).broadcast_to([128, 2, 128])
                op("pool", lambda: POOL.tensor_tensor(out=v3(wt1[k]), in0=v3(wxg[k]), in1=Cb, op=ALU.mult),
                   reads=[t_wxg[k], t_rope[r]], writes=[t_wt1[k]])
                xv = wxg[k][:].rearrange("p (a s j) -> p a s j", a=4, s=2)
                ov = wt2[k][:].rearrange("p (a s j) -> p a s j", a=4, s=2)
                sv = rt[:, 1, :].rearrange("p (a s j) -> p a s j", a=2, s=2)
                first = True
                for hh in range(2):
                    for s_ in range(2):
                        op("pool", lambda hh=hh, s_=s_: POOL.tensor_tensor(
                            out=ov[:, 2 * hh:2 * hh + 2, s_, :], in0=xv[:, 2 * hh:2 * hh + 2, 1 - s_, :],
                            in1=sv[:, :, s_, :], op=ALU.mult),
                           reads=[t_wxg[k], t_rope[r]] if first else [], writes=[t_wt2[k]] if first else [])
                        first = False
                mark((ctr["pool"], ctr["pool"].n), reads=[t_wxg[k], t_rope[r]], writes=[t_wt2[k]])
                op("pool", lambda: POOL.tensor_tensor(out=wt1[k][:], in0=wt1[k][:], in1=wt2[k][:], op=ALU.add),
                   reads=[t_wt1[k], t_wt2[k]], writes=[t_wt1[k]])
                item["cur"] = (wt1[k], t_wt1[k])

        def qk_final(item):
            k = item["k"]
            cur, t_cur = item["cur"]
            q_ = state["qi"] % 2
            state["qi"] += 1
            rb = wst[k][:, 4:6].unsqueeze(2).broadcast_to([128, 2, 128])
            op("dve", lambda: DVE.tensor_tensor(out=qtok[q_][:].rearrange("p (h d) -> p h d", h=2),
                                                in0=cur[:].rearrange("p (h d) -> p h d", h=2), in1=rb, op=ALU.mult),
               reads=[t_cur, t_wst[k]], writes=[t_qtok[q_]])
            m = 1
            for h in range(2):
                op("pe", lambda h=h: PE.transpose(out=ps_m[:, m, h * 128:(h + 1) * 128], in_=qtok[q_][:, h * 128:(h + 1) * 128],
                                                  identity=identb[:]),
                   reads=[t_qtok[q_]] if h == 0 else [], writes=[t_ps_m[m]] if h == 0 else [], inc=(h == 1))
            mark(pe_ev(), reads=[t_qtok[q_]], writes=[t_ps_m[m]])
            ncols = item.get("ncols", 128)
            dst_fn, t_dst = item["dst_fn"], item["t_dst"]
            for h in range(2):
                op("dve", lambda h=h: DVE.tensor_copy(out=dst_fn(h), in_=ps_m[:, m, h * 128:h * 128 + ncols]),
                   reads=[t_ps_m[m]], writes=[t_dst])

        def load_rope(item, src_ap):
            r = state["ri"] % 2
            state["ri"] += 1
            item["rope"] = r
            op("sp", lambda: SP.dma_start(out=rope_t[r][:], in_=src_ap), writes=[t_rope[r]], dma=d_rope[r])

        def fin_dve(item):
            j = item["acc"]
            a = state["oi"] % 2
            state["oi"] += 1
            item["og"] = a
            acc = pso(j)
            op("dve", lambda: DVE.reciprocal(out=rl[a][:], in_=acc[:, 128:129]), reads=[t_ps_o[j // 2]], writes=[t_rl[a]])
            op("dve", lambda: DVE.scalar_tensor_tensor(out=og[a][:], in0=acc[:, 0:128], scalar=rl[a][:, 0:1], in1=item["zsrc"],
                                                       op0=ALU.mult, op1=ALU.mult),
               reads=[t_ps_o[j // 2], t_rl[a], item.get("t_z", t_ZG)], writes=[t_og[a]])

        def fin_tr(item):
            a = item["og"]
            m = 1
            chunk, tokcol = item["och"], item["otok"]
            op("pe", lambda: PE.transpose(out=ps_m[:, m, 0:128], in_=og[a][:], identity=identb[:]),
               reads=[t_og[a]], writes=[t_ps_m[m]])
            op("dve", lambda: DVE.tensor_copy(out=OT[:, chunk, tokcol:tokcol + 128], in_=ps_m[:, m, 0:128]),
               reads=[t_ps_m[m]], writes=[t_OT])

        class _Stop(Exception):
            pass
        STOP = os.environ.get("KSTOP", "")

        def stop(name):
            if STOP == name:
                raise _Stop()

        def main():
            chunk_id = 0
            for (nt, tag) in SEQS:
                nkc = nt + 1
                wq = WQ([gidx_in(C_KA), gidx_in(C_VA)])
                wka, wva = wq.get(0), wq.get(1)
                items = []
                for t in range(nt + 1):
                    meta = (t == nt)
                    items.append({
                        "t": t, "npart": 16 if meta else 128, "ncols": 16 if meta else 128,
                        "src": xloc[0, 8 * 128:9 * 128, :] if meta else xall[tag][t * 128:(t + 1) * 128, :],
                        "dst": xnTl[:, :, (t % 2) * 128:(t % 2 + 1) * 128], "t_dst": t_xnTl[t % 2],
                        "gidx": 1, "t_dstk": t_KAT, "rs": (st[t % 4][:, :], t_st[t % 4]),
                    })
                for it in items:
                    it["dst_fn"] = (lambda h, it=it: KAT[:, h, it["t"] * 128:it["t"] * 128 + it["npart"]])
                    it["t_dst_k"] = t_KAT

                def kv_proj(it):
                    t = it["t"]
                    o = t % 2
                    npart = it["npart"]
                    load_rope(it, ropekv[t if t < nt else 64, :, :, :])
                    pk = next_pu()
                    proj(lambda c: xnTl[:, c, o * 128:(o + 1) * 128], t_xnTl[o], wka, pk)
                    qk_evac(it, pk)
                    pv = next_pu()
                    proj(lambda c: xnTl[:, c, o * 128:(o + 1) * 128], t_xnTl[o], wva, pv)
                    rs, t_rs = it["rs"]
                    op("dve", lambda: DVE.tensor_scalar(out=VAE[:npart, t, :, 0:128],
                                                        in0=psu(pv)[:npart, :].rearrange("p (h d) -> p h d", h=2),
                                                        scalar1=rs[:npart, 2:3], scalar2=None, op0=ALU.mult),
                       reads=[t_bk[pv], t_rs], writes=[t_VAE])

                def kv_final(it):
                    it2 = dict(it)
                    it2["t_dst"] = t_KAT
                    it2["k"] = it["k"]
                    it2["cur"] = it["cur"]
                    qk_final(it2)

                run_pipe(items, [(0, nt_load), (1, nt_norm), (2, nt_tr), (3, kv_proj), (4, qk_stats), (5, kv_final)])
                wq.flush()
                stop("kv")
                for q in range(2):
                    ck = chunk_id
                    chunk_id += 1
                    LT_ORDER = [2, 3, 4, 5, 0, 1, 6, 7, 8]
                    nitems = [{"src": xloc[ck, lt * 128:(lt + 1) * 128, :], "dst": xnTl[:, :, lt * 128:(lt + 1) * 128],
                               "t_dst": t_xnTl[lt], "rs": (rloc[:, lt, :], t_rloc[lt])} for lt in LT_ORDER]
                    gnorm = gen_pipe(nitems, [(0, nt_load), (1, nt_norm), (2, nt_tr)])
                    groups = []
                    bgr = lambda hp: [gidx_in(C_QB) + hp, gidx_in(C_KB) + hp, gidx_in(C_VB) + hp, gidx_in(C_ZB) + hp]
                    groups += bgr(0)
                    for hp in range(4):
                        groups += [gidx_in(C_QA) + hp, gidx_in(C_ZA) + hp]
                        if hp < 3:
                            groups += bgr(hp + 1)
                    groups += [26 + n for n in range(8)]
                    wq = WQ(groups)
                    gpos = [0]

                    def nextw():
                        s_ = wq.get(gpos[0])
                        gpos[0] += 1
                        return s_

                    def xl(lt):
                        return lambda c, lt=lt: xnTl[:, c, lt * 128:(lt + 1) * 128]

                    def ip_proj(it):
                        lt = it["lt"]
                        it["rs"] = (rloc[:, lt, :], t_rloc[lt])
                        rs, t_rs = it["rs"]
                        pu = next_pu()
                        if it["kind"] == "qa":
                            load_rope(it, ropeq[ck * 4 + it["j"], :, :, :])
                        proj(xl(lt), t_xnTl[lt], it["w"], pu)
                        kind = it["kind"]
                        if kind in ("qb", "kb", "qa"):
                            qk_evac(it, pu)
                        elif kind == "vb":
                            op("dve", lambda: DVE.tensor_scalar(out=VBE[:, lt, :, 0:128],
                                                                in0=psu(pu).rearrange("p (h d) -> p h d", h=2),
                                                                scalar1=rs[:, 2:3], scalar2=None, op0=ALU.mult),
                               reads=[t_pu(pu), t_rs], writes=[t_VBE])
                        else:
                            zdst, t_zdst = (ZG, t_ZG) if kind == "zb" else (it["zga"], it["t_zga"])
                            k = state["ki"] % 2
                            state["ki"] += 1
                            op("act", lambda: ACTE.activation(out=wk[k][:], in_=psu(pu), func=ACT.Exp, scale=rs[:, 3:4]),
                               reads=[t_pu(pu), t_rs], writes=[t_wk[k]])
                            op("act", lambda: ACTE.activation(out=wk[k][:], in_=wk[k][:], func=ACT.Ln, bias=onest[:]),
                               reads=[t_wk[k]], writes=[t_wk[k]])
                            op("act", lambda: ACTE.activation(out=wk[k][:], in_=wk[k][:], func=ACT.Exp, scale=-1.0),
                               reads=[t_wk[k]], writes=[t_wk[k]])
                            op("dve", lambda: DVE.scalar_tensor_tensor(out=zdst[:, it["j"], :], in0=psu(pu), scalar=rs[:, 2:3],
                                                                       in1=wk[k][:], op0=ALU.mult, op1=ALU.mult),
                               reads=[t_pu(pu), t_wk[k], t_rs], writes=[t_zdst])

                    def ip_stats(it):
                        if it["kind"] in ("qb", "kb", "qa"):
                            qk_stats(it)

                    def ip_final(it):
                        if it["kind"] in ("qb", "kb", "qa"):
                            qk_final(it)

                    def gen_bproj(hp):
                        wq_, wk_, wv_, wz_ = nextw(), None, None, None
                        for b in range(4):
                            bi = b
                            for qa in range(2):
                                base = ((((ck * 4 + b) * 2 + qa) * 8 + hp * 2) * 10) * 128
                                src = AP(rw.tensor, base, [[1, 64], [10 * 128, 2], [128, 10], [1, 64]])
                                dst = BT[bi][qa * 64:(qa + 1) * 64, :, :].rearrange("p h (r c) -> p h r c", r=10)
                                op("pool", lambda src=src, dst=dst: POOL.dma_start(out=dst, in_=src),
                                   writes=[t_BT[bi]] if qa == 0 else [], dma=d_bt[bi])
                            t_BT[bi].w = (d_bt[bi], d_bt[bi].n)
                            op("pool", lambda bi=bi: POOL.tensor_tensor(
                                out=BT[bi][:].rearrange("p h (r c) -> p (h r) c", r=10),
                                in0=BT[bi][:].rearrange("p h (r c) -> p (h r) c", r=10),
                                in1=cmb[:, :].unsqueeze(1).broadcast_to([128, 20, 64]), op=ALU.add),
                               writes=[t_BT[bi]])
                        items = []
                        for j in range(4):
                            items.append({"kind": "qb", "lt": 2 + j, "j": j, "w": wq_, "gidx": 2, "t_dst": t_QT,
                                          "dst_fn": (lambda h, j=j: QT[:, h, j * 128:(j + 1) * 128])})
                        yield from gen_pipe(items, [(3, ip_final), (0, ip_proj), (1, ip_stats)])
                        wk_ = nextw()
                        items = []
                        for lt in LT_ORDER:
                            items.append({"kind": "kb", "lt": lt, "w": wk_, "gidx": 3, "t_dst": t_KBT,
                                          "dst_fn": (lambda h, lt=lt: KBT[:, h, lt * 128:(lt + 1) * 128])})
                        yield from gen_pipe(items, [(3, ip_final), (0, ip_proj), (1, ip_stats)])
                        wv_ = nextw()
                        items = [{"kind": "vb", "lt": lt, "w": wv_} for lt in LT_ORDER]
                        yield from gen_pipe(items, [(0, ip_proj)])
                        wz_ = nextw()
                        items = [{"kind": "zb", "lt": 2 + j, "j": j, "w": wz_} for j in range(4)]
                        yield from gen_pipe(items, [(0, ip_proj)])

                    def battn(hp):
                        items = []
                        for b in range(4):
                            for hl in range(2):
                                items.append({"b": b, "hl": hl, "zsrc": ZG[:, b, hl * 128:(hl + 1) * 128],
                                              "och": 8 + hp * 2 + hl, "otok": b * 128})

                        def b_bt(it):
                            it["bt"] = it["b"]

                        def b_scores(it):
                            b, hl = it["b"], it["hl"]
                            if hl == 1:
                                it["bt"] = items[items.index(it) - 1]["bt"]
                            bi = it["bt"]
                            sp_ = state["si"] % 2
                            state["si"] += 1
                            it["sp"] = sp_
                            B0, B1 = 2 * sp_, 2 * sp_ + 1
                            for kt in range(5):
                                bank = B0 if kt < 4 else B1
                                outp = psA[:, bank, (kt % 4) * 128:(kt % 4) * 128 + 128]
                                first = (kt == 0)
                                op("pe", lambda kt=kt, outp=outp: PE.matmul(
                                    outp, lhsT=KBT[:, hl, (b + kt) * 128:(b + kt + 1) * 128], rhs=QT[:, hl, b * 128:(b + 1) * 128],
                                    start=True, stop=False),
                                   reads=[t_KBT, t_QT] if first else [], writes=[t_bk[B0], t_bk[B1]] if first else [], inc=False)
                                op("pe", lambda kt=kt, outp=outp: PE.matmul(
                                    outp, lhsT=BT[bi][:, hl, kt * 128:(kt + 1) * 128], rhs=prevb[:], start=False, stop=True),
                                   reads=[t_BT[bi]] if first else [], inc=False)
                            op("pe", lambda: PE.matmul(psA[0:16, B1, 128:256], lhsT=KBT[:, hl, 8 * 128:8 * 128 + 16],
                                                       rhs=QT[:, hl, b * 128:(b + 1) * 128], start=True, stop=True))
                            mark(pe_ev(), reads=[t_KBT, t_QT, t_BT[bi]], writes=[t_bk[B0], t_bk[B1]])
                            p_ = state["pi"] % 2
                            state["pi"] += 1
                            it["pt"] = p_
                            op("act", lambda: ACTE.activation(out=PT[p_][:, 0:512], in_=psA[:, B0, :], func=ACT.Exp, bias=negc[:, 1:2]),
                               reads=[t_bk[B0], t_negc], writes=[t_PT[p_]])
                            op("act", lambda: ACTE.activation(out=PT[p_][:, 512:640], in_=psA[:, B1, 0:128], func=ACT.Exp, bias=negc[:, 1:2]),
                               reads=[t_bk[B1]], writes=[t_PT[p_]])
                            op("act", lambda: ACTE.activation(out=PT[p_][0:16, 640:768], in_=psA[0:16, B1, 128:256], func=ACT.Exp,
                                                              bias=negc[0:16, 1:2]),
                               reads=[t_bk[B1]], writes=[t_PT[p_]])

                        def b_pv(it):
                            b, hl, p_ = it["b"], it["hl"], it["pt"]
                            j = (state["oi2"] % 2) * 2 if "oi2" in state else 0
                            state["oi2"] = state.get("oi2", 0) + 1
                            it["acc"] = j
                            for kt in range(5):
                                op("pe", lambda kt=kt: PE.matmul(
                                    pso(j), lhsT=PT[p_][:, kt * 128:(kt + 1) * 128], rhs=VBE[:, b + kt, hl, 0:129],
                                    start=(kt == 0), stop=False),
                                   reads=[t_PT[p_], t_VBE] if kt == 0 else [], writes=[t_ps_o[j // 2]] if kt == 0 else [], inc=False)
                            op("pe", lambda: PE.matmul(pso(j), lhsT=PT[p_][0:16, 640:768], rhs=VBE[0:16, 8, hl, 0:129],
                                                       start=False, stop=True))
                            mark(pe_ev(), reads=[t_PT[p_], t_VBE], writes=[t_ps_o[j // 2]])
                            fin_dve(it)

                        n_it = len(items)
                        for s_ in range(-2, n_it + 2):
                            if 0 <= s_ + 2 < n_it:
                                b_bt(items[s_ + 2])
                            if 0 <= s_ < n_it:
                                b_scores(items[s_])
                            if 0 <= s_ - 1 < n_it:
                                b_pv(items[s_ - 1])
                            if 0 <= s_ - 2 < n_it:
                                fin_tr(items[s_ - 2])

                    def gen_aproj(hp):
                        QTA, t_QTA, ZGA, t_ZGA = QTAs[hp % 2], t_QTAs[hp % 2], ZGAs[hp % 2], t_ZGAs[hp % 2]
                        wq_ = nextw()
                        items = []
                        for j in range(4):
                            items.append({"kind": "qa", "lt": 2 + j, "j": j, "w": wq_, "gidx": 0, "t_dst": t_QTA,
                                          "dst_fn": (lambda h, j=j: QTA[:, h, j * 128:(j + 1) * 128])})
                        yield from gen_pipe(items, [(3, ip_final), (0, ip_proj), (1, ip_stats)])
                        wz_ = nextw()
                        items = [{"kind": "za", "lt": 2 + j, "j": j, "w": wz_, "zga": ZGA, "t_zga": t_ZGA} for j in range(4)]
                        yield from gen_pipe(items, [(0, ip_proj)])

                    def gen_next(hp):
                        yield from gen_bproj(hp)
                        yield from gen_aproj(hp)

                    def aphase(hp, genB, nsteps):
                        g = hp // 2
                        QTA, t_QTA, ZGA, t_ZGA = QTAs[hp % 2], t_QTAs[hp % 2], ZGAs[hp % 2], t_ZGAs[hp % 2]
                        if genB is not None:
                            state["pu_mode"] = "single"
                        ucount = [0]
                        done_steps = [0]
                        nunits_tot = 2 * (nt // 2 + 1)
                        for hl in range(2):
                            units = [[2 * p, 2 * p + 1] for p in range(nt // 2)] + [[nt]]

                            def S(u):
                                cs = units[u]
                                sp_ = u % 2
                                p_ = state["pi"] % 2
                                state["pi"] += 1
                                if len(cs) == 2:
                                    for i_, c in enumerate(cs):
                                        bank = 2 * sp_ + i_
                                        op("pe", lambda c=c, bank=bank: PE.matmul(psA[:, bank, :], lhsT=KAT[:, g, c * 128:(c + 1) * 128],
                                                                                  rhs=QTA[:, hl, :], start=True, stop=True),
                                           reads=[t_KAT, t_QTA] if i_ == 0 else [],
                                           writes=[t_bk[2 * sp_], t_bk[2 * sp_ + 1]] if i_ == 0 else [], inc=(i_ == 1))
                                    mark(pe_ev(), reads=[t_KAT, t_QTA], writes=[t_bk[2 * sp_], t_bk[2 * sp_ + 1]])
                                    op("act", lambda: ACTE.activation(out=PT[p_][:].rearrange("p (a n) -> p a n", a=2),
                                                                      in_=psA[:, 2 * sp_:2 * sp_ + 2, :], func=ACT.Exp, bias=negc[:, 0:1]),
                                       reads=[t_bk[2 * sp_], t_bk[2 * sp_ + 1], t_negc], writes=[t_PT[p_]])
                                else:
                                    c = cs[0]
                                    bank = 2 * sp_
                                    op("pe", lambda: PE.matmul(psA[:16, bank, :], lhsT=KAT[:, g, c * 128:c * 128 + 16], rhs=QTA[:, hl, :],
                                                               start=True, stop=True),
                                       reads=[t_KAT, t_QTA], writes=[t_bk[bank]])
                                    op("act", lambda: ACTE.activation(out=PT[p_][:16, 0:512], in_=psA[:16, bank, :], func=ACT.Exp,
                                                                      bias=negc[:16, 0:1]),
                                       reads=[t_bk[bank], t_negc], writes=[t_PT[p_]])
                                return p_

                            def PV(u, p_):
                                cs = units[u]
                                last_u = (u == len(units) - 1)
                                for i_, c in enumerate(cs):
                                    kc = 128 if c < nt else 16
                                    for j in range(4):
                                        firstmm = (u == 0 and i_ == 0)
                                        lastmm = (last_u and i_ == len(cs) - 1)
                                        op("pe", lambda j=j, c=c, i_=i_, kc=kc: PE.matmul(
                                            pso(j), lhsT=PT[p_][:kc, i_ * 512 + j * 128:i_ * 512 + (j + 1) * 128],
                                            rhs=VAE[:kc, c, g, 0:129], start=(firstmm and j % 2 == 0), stop=lastmm,
                                            skip_group_check=True),
                                           reads=[t_PT[p_], t_VAE] if (i_ == 0 and j == 0) else [],
                                           writes=[t_ps_o[0], t_ps_o[1]] if (firstmm and j == 0) else [],
                                           inc=(i_ == len(cs) - 1 and j == 3))
                                mark(pe_ev(), reads=[t_PT[p_], t_VAE], writes=[t_ps_o[0], t_ps_o[1]] if last_u else [])

                            pend = S(0)
                            for u in range(len(units)):
                                nxt = S(u + 1) if u + 1 < len(units) else None
                                PV(u, pend)
                                pend = nxt
                                ucount[0] += 1
                                if genB is not None:
                                    want = min((ucount[0] * nsteps) // nunits_tot, done_steps[0] + 1)
                                    while done_steps[0] < want:
                                        next(genB, None)
                                        done_steps[0] += 1
                            fitems = [{"acc": j, "zsrc": ZGA[:, j, hl * 128:(hl + 1) * 128], "t_z": t_ZGA,
                                       "och": hp * 2 + hl, "otok": j * 128}
                                      for j in range(4)]
                            run_pipe(fitems, [(0, fin_dve), (1, fin_tr)])

                    g0 = gen_next(0)
                    step_ = 0
                    for _ in gnorm:
                        if step_ >= 4:
                            next(g0, None)
                        step_ += 1
                    for _ in g0:
                        pass
                    for hp in range(4):
                        battn(hp)
                        genB = gen_next(hp + 1) if hp < 3 else None
                        aphase(hp, genB, 46)
                        state["pu_mode"] = "double"
                        if genB is not None:
                            for _ in genB:
                                pass
                    stop("A")
                    for n in range(8):
                        wo = nextw()
                        for j in range(4):
                            a = state["yi"] % 2
                            state["yi"] += 1
                            op("pool", lambda: POOL.dma_start(out=xres[a][:], in_=xloc[ck, (2 + j) * 128:(3 + j) * 128, n * 256:(n + 1) * 256]),
                               writes=[t_xres[a]], dma=d_xres[a])
                            pu = next_pu()
                            proj(lambda c, j=j: OT[:, c, j * 128:(j + 1) * 128], t_OT, wo, pu)
                            op("dve", lambda: DVE.tensor_tensor(out=ysb[a][:], in0=psu(pu), in1=xres[a][:], op=ALU.add),
                               reads=[t_bk[pu], t_xres[a]], writes=[t_ysb[a]])
                            op("act", lambda: ACTE.dma_start(out=yout[ck, j * 128:(j + 1) * 128, n * 256:(n + 1) * 256], in_=ysb[a][:]),
                               reads=[t_ysb[a]], dma=d_out[a])
                    wq.flush()
        try:
            main()
        except _Stop:
            pass
        SP.wait_ge(d_out[0].h, d_out[0].n)
        SP.wait_ge(d_out[1].h, d_out[1].n)
    return nc, used


def _rope_tables(rows, cols):
    half = 64
    inv = (10000.0 ** (-np.arange(0, half, 2, dtype=np.float32) / half)).astype(np.float32)
    ar = rows.astype(np.float32)[:, None] * inv[None, :]
    ac = cols.astype(np.float32)[:, None] * inv[None, :]
    cr, sr, cc, sc = np.cos(ar), np.sin(ar), np.cos(ac), np.sin(ac)
    C = np.concatenate([cr, cr, cc, cc], axis=1)
    S = np.concatenate([-sr, sr, -sc, sc], axis=1)
    return np.stack([C, S], axis=1).astype(np.float32)


def _tile_rope(T):
    p = np.arange(128)
    return _rope_tables(2 * T + p // 64, p % 64)


def _meta_rope():
    r = np.zeros((128,), np.int64) - 1
    c = np.arange(128) % 16
    return _rope_tables(r, c)


_NC_CACHE = {}


def kernel(x_prompt, x_sample, meta_tokens, norm_w, w_in, q_norm_a, k_norm_a, q_norm_b, k_norm_b, rpb, w_out):
    f32 = np.float32
    x_prompt = np.asarray(x_prompt, f32); x_sample = np.asarray(x_sample, f32)
    meta_tokens = np.asarray(meta_tokens, f32)
    w_in2 = np.ascontiguousarray(np.asarray(w_in, f32)[0]); w_out2 = np.ascontiguousarray(np.asarray(w_out, f32)[0])
    rpb2 = np.asarray(rpb, f32)[0]
    normw_t = np.ascontiguousarray(np.asarray(norm_w, f32)[0].reshape(16, 128).T)
    gains = np.stack([np.asarray(q_norm_a, f32)[0], np.asarray(k_norm_a, f32)[0],
                      np.asarray(q_norm_b, f32)[0], np.asarray(k_norm_b, f32)[0]]).astype(f32)
    ident = np.eye(128, dtype=f32)
    prev = np.zeros((128, 128), f32)
    for qa in range(2):
        for c in range(64):
            prev[qa * 64 + c, qa * 64 + 63 - c] = 1.0
    cm = np.full((128, 64), NEG, f32)
    for qa in range(2):
        for qcc in range(64):
            qc = 63 - qcc
            c0 = min(max(qc - 8, 0), 48)
            cm[qa * 64 + qcc, c0:c0 + 16] = 0.0
    ropekv = np.stack([_tile_rope(T) for T in range(64)] + [_meta_rope()]).astype(f32)
    meta_tile = np.zeros((128, D), f32); meta_tile[:16] = meta_tokens

    in_maps = []
    for i in range(NCORES):
        si, hi = i // 2, i % 2
        chunks = [("p", x_prompt[0], 64, 8 * i), ("p", x_prompt[0], 64, 8 * i + 4),
                  ("s", x_sample[si], 16, 8 * hi), ("s", x_sample[si], 16, 8 * hi + 4)]
        xloc = np.zeros((4, 9 * 128, D), f32)
        rwa = np.full((4, 4, 2, 8, 10, 128), NEG, f32)
        ropeq = np.zeros((16, 128, 2, 128), f32)
        for ck, (tag, xs, nb, T0) in enumerate(chunks):
            rows = 2 * nb
            gt_of = [None] * 8
            is_copy = [False] * 8
            for L in range(8):
                G = T0 - 2 + L
                if 0 <= G < nb:
                    gt_of[L] = G
            if T0 == 0:
                gt_of[1] = 3; is_copy[1] = True
            if T0 + 4 == nb:
                gt_of[6] = nb - 4; is_copy[6] = True
            for L in range(8):
                if gt_of[L] is not None:
                    xloc[ck, L * 128:(L + 1) * 128] = xs[gt_of[L] * 128:(gt_of[L] + 1) * 128]
            xloc[ck, 8 * 128:9 * 128] = meta_tile
            for b in range(4):
                Tb = T0 + b
                ropeq[ck * 4 + b] = ropekv[Tb]
                slots = list(range(b, b + 5))
                originals = {gt_of[L] for L in slots if gt_of[L] is not None and not is_copy[L]}
                for qa in range(2):
                    qr = 2 * Tb + qa
                    r0 = min(max(qr - 4, 0), rows - 8)
                    for kr_rel in range(10):
                        L = b + kr_rel // 2
                        G = gt_of[L]
                        if G is None:
                            continue
                        if is_copy[L] and G in originals:
                            continue
                        kr = 2 * G + kr_rel % 2
                        if r0 <= kr < r0 + 8:
                            dr = kr - qr + 7
                            rwa[ck, b, qa, :, kr_rel, 48:79] = rpb2[:, dr, :]
        in_maps.append({
            "xall_p": x_prompt[0], "xall_s": x_sample[si], "xloc": xloc, "ropekv": ropekv, "ropeq": ropeq,
            "w_in": w_in2, "w_out": w_out2, "normw_t": normw_t, "gains": gains, "identd": ident, "prevd": prev,
            "rw": rwa, "cmd": cm,
        })
    if "nc" not in _NC_CACHE:
        _NC_CACHE["nc"] = build_program()
    nc = _NC_CACHE["nc"]
    res = run_bass_kernel_spmd(nc, in_maps, core_ids=list(range(NCORES)))
    y_prompt = np.zeros((1, 8192, D), f32)
    y_sample = np.zeros((4, 2048, D), f32)
    for i in range(NCORES):
        y = np.asarray(res.results[i]["y"], f32)
        si, hi = i // 2, i % 2
        y_prompt[0, 1024 * i:1024 * i + 512] = y[0]
        y_prompt[0, 1024 * i + 512:1024 * (i + 1)] = y[1]
        y_sample[si, 1024 * hi:1024 * hi + 512] = y[2]
        y_sample[si, 1024 * hi + 512:1024 * (hi + 1)] = y[3]
    return (y_prompt, y_sample)
```

```python
import contextlib
import os
import numpy as np
import concourse.bass as bass
import concourse.mybir as mybir
from concourse.ap import AP
from concourse.bass_utils import run_bass_kernel_spmd

F32 = mybir.dt.float32
BF16 = mybir.dt.bfloat16
ACT = mybir.ActivationFunctionType
ALU = mybir.AluOpType
AX = mybir.AxisListType

D = 2048
DIN = 6656
HD = 128
NEG = -30000.0
EPS = 1e-6
NCORES = 8
C_QA, C_KA, C_VA, C_ZA, C_QB, C_KB, C_VB, C_ZB = 0, 1024, 1280, 1536, 2560, 3584, 4608, 5632
SEQS = [(64, "p"), (16, "s")]


class Ctr:
    def __init__(self, nc, es, name):
        self.h = es.enter_context(nc.semaphore(name))
        self.n = 0
        self.nn = 0
        self.map = {}
        self.name = name


class Tk:
    __slots__ = ("w", "r")

    def __init__(self):
        self.w = None
        self.r = []


def build_program():
    _, used = _build(None)
    nc, _ = _build(used)
    return nc


def _build(needed):
    nc = bass.Bass("TRN2", target_bir_lowering=False)
    used = {}
    es = contextlib.ExitStack()

    def dram(name, shape, kind="ExternalInput", dt=F32):
        return nc.dram_tensor(name, list(shape), dt, kind=kind).ap()

    xall = {"p": dram("xall_p", [8192, D]), "s": dram("xall_s", [2048, D])}
    xloc = dram("xloc", [4, 9 * 128, D])
    ropekv = dram("ropekv", [65, 128, 2, 128])
    ropeq = dram("ropeq", [16, 128, 2, 128])
    w_in = dram("w_in", [D, DIN])
    w_out = dram("w_out", [D, D])
    normw_t = dram("normw_t", [128, 16])
    gains = dram("gains", [4, 128])
    identd = dram("identd", [128, 128])
    prevd = dram("prevd", [128, 128])
    rw = dram("rw", [4, 4, 2, 8, 10, 128])
    cmd = dram("cmd", [128, 64])
    yout = dram("y", [4, 512, D], kind="ExternalOutput")
    wbf = dram("wbf", [34, 128, 16 * 256], kind="Internal", dt=BF16)

    sb_total = [0]

    def sb(name, shape, dt):
        n = 1
        for d_ in shape[1:]:
            n *= d_
        sb_total[0] += n * (4 if dt == F32 else 2)
        return es.enter_context(nc.sbuf_tensor(name, list(shape), dt))

    def ps(name, shape, dt):
        return es.enter_context(nc.psum_tensor(name, list(shape), dt))

    with es:
        PE, ACTE, DVE, POOL, SP = nc.tensor, nc.scalar, nc.vector, nc.gpsimd, nc.sync
        ctr = {n: Ctr(nc, es, n) for n in ("pe", "act", "dve", "pool")}
        eng_of = {"pe": PE, "act": ACTE, "dve": DVE, "pool": POOL, "sp": SP}
        waited = {}

        def wait(engn, ev):
            if ev is None:
                return
            c, v = ev
            key = (engn, c.name)
            if waited.get(key, 0) >= v:
                return
            waited[key] = v
            if c.name in ctr:
                used.setdefault(c.name, set()).add(v)
                eng_of[engn].wait_ge(c.h, c.map[v])
            else:
                eng_of[engn].wait_ge(c.h, v)

        def op(engn, fn, reads=(), writes=(), inc=True, dma=None):
            for t in reads:
                wait(engn, t.w)
            for t in writes:
                wait(engn, t.w)
                for e in t.r:
                    wait(engn, e)
            ins = fn()
            if dma is not None:
                ins.then_inc(dma.h, 16)
                dma.n += 16
                ev = (dma, dma.n)
            elif inc:
                c = ctr[engn]
                c.n += 1
                if needed is None or c.n in needed.get(c.name, ()):
                    ins.then_inc(c.h, 1)
                    c.nn += 1
                c.map[c.n] = c.nn
                ev = (c, c.n)
            else:
                return None
            mark(ev, reads, writes)
            return ev

        def mark(ev, reads=(), writes=()):
            for t in reads:
                t.r = [e for e in t.r if e[0] is not ev[0]] + [ev]
            for t in writes:
                t.w = ev
                t.r = []

        def pe_ev():
            return (ctr["pe"], ctr["pe"].n)

        def dmasem(name):
            return Ctr(nc, es, name)

        identb = sb("identb", [128, 128], BF16)
        prevb = sb("prevb", [128, 128], BF16)
        normw = sb("normw", [128, 16], F32)
        gt = sb("gt", [128, 4, 128], F32); t_gt = Tk()
        gmx = sb("gmx", [128, 4], F32)
        negc = sb("negc", [128, 2], F32); t_negc = Tk()
        epst = sb("epst", [128, 1], F32)
        onest = sb("onest", [128, 1], F32)
        cmb = sb("cmb", [128, 64], BF16); t_cmb = Tk()
        NWS = 2
        ws = [sb(f"ws{i}", [128, 16, 256], BF16) for i in range(NWS)]; t_ws = [Tk() for _ in range(NWS)]
        xf = [sb(f"xf{i}", [128, D], F32) for i in range(2)]; t_xf = [Tk(), Tk()]
        xb = [sb(f"xb{i}", [128, D], BF16) for i in range(2)]; t_xb = [Tk(), Tk()]
        st = [sb(f"st{i}", [128, 4], F32) for i in range(4)]; t_st = [Tk() for _ in range(4)]
        rloc = sb("rloc", [128, 9, 4], F32); t_rloc = [Tk() for _ in range(9)]
        xnTl = sb("xnTl", [128, 16, 9 * 128], BF16); t_xnTl = [(Tk(), Tk()) for _ in range(9)]
        OT = sb("OT", [128, 16, 512], BF16); t_OT = Tk()
        KAT = sb("KAT", [128, 2, 8208], BF16); t_KAT = Tk()
        VAE = sb("VAE", [128, 65, 2, 130], BF16); t_VAE = Tk()
        rope_t = [sb(f"rope{i}", [128, 2, 128], F32) for i in range(2)]; t_rope = [Tk(), Tk()]
        wk = [sb(f"wk{i}", [128, 256], F32) for i in range(2)]; t_wk = [Tk(), Tk()]
        wxg = [sb(f"wxg{i}", [128, 256], F32) for i in range(2)]; t_wxg = [Tk(), Tk()]
        wt1 = [sb(f"wt1{i}", [128, 256], F32) for i in range(2)]; t_wt1 = [Tk(), Tk()]
        wt2 = [sb(f"wt2{i}", [128, 256], F32) for i in range(2)]; t_wt2 = [Tk(), Tk()]
        wst = [sb(f"wst{i}", [128, 8], F32) for i in range(2)]; t_wst = [Tk(), Tk()]
        qtok = [sb(f"qtok{i}", [128, 256], BF16) for i in range(2)]; t_qtok = [Tk(), Tk()]
        QT = sb("QT", [128, 2, 512], BF16); t_QT = Tk()
        KBT = sb("KBT", [128, 2, 9 * 128], BF16); t_KBT = Tk()
        VBE = sb("VBE", [128, 9, 2, 130], BF16); t_VBE = Tk()
        ZG = sb("ZG", [128, 4, 256], BF16); t_ZG = Tk()
        QTAs = [sb(f"QTA{i}", [128, 2, 512], BF16) for i in range(2)]; t_QTAs = [Tk(), Tk()]
        ZGAs = [sb(f"ZGA{i}", [128, 4, 256], BF16) for i in range(2)]; t_ZGAs = [Tk(), Tk()]
        BT = [sb(f"BT{i}", [128, 2, 640], BF16) for i in range(4)]; t_BT = [Tk() for _ in range(4)]
        PT = [sb(f"PT{i}", [128, 1024], BF16) for i in range(2)]; t_PT = [Tk(), Tk()]
        rl = [sb(f"rl{i}", [128, 1], F32) for i in range(2)]; t_rl = [Tk(), Tk()]
        og = [sb(f"og{i}", [128, 128], BF16) for i in range(2)]; t_og = [Tk(), Tk()]
        xres, t_xres = wxg, t_wxg
        ysb, t_ysb = wt1, t_wt1
        if os.environ.get("KDEBUG"):
            print("SBUF bytes/partition allocated:", sb_total[0], "remaining:", nc.sbuf_bytes_remaining)
        psA = ps("psA", [128, 4, 512], F32); t_bk = [Tk() for _ in range(4)]
        ps_m = ps("ps_m", [128, 2, 1024], BF16); t_ps_m = [Tk(), Tk()]
        ps_o = ps("ps_o", [128, 2, 512], F32); t_ps_o = [Tk(), Tk()]

        def pst(hf, alt=0):
            src = psA[:, 2 + hf, :] if alt == 0 else ps_o[:, hf, :]
            return src.bitcast(BF16).rearrange("p (c t) -> p c t", c=8)

        def t_pst(hf, alt=0):
            return t_bk[2 + hf] if alt == 0 else t_ps_o[hf]

        def psu(pu):
            if pu == 2:
                return ps_m[:, 0, :].bitcast(F32)[:, 0:256]
            return psA[:, pu, 0:256]

        def t_pu(pu):
            return t_ps_m[0] if pu == 2 else t_bk[pu]

        def pso(j):
            return ps_o[:, j // 2, (j % 2) * 256:(j % 2) * 256 + 129]

        d_c = dmasem("d_const"); d_cp = dmasem("d_constp")
        d_xf = [dmasem("d_xf0"), dmasem("d_xf1")]
        d_wsp = [dmasem(f"d_wsp{i}") for i in range(NWS)]
        d_wss = [dmasem(f"d_wss{i}") for i in range(NWS)]
        d_wst = [dmasem(f"d_wst{i}") for i in range(NWS)]
        d_rope = [dmasem("d_rope0"), dmasem("d_rope1")]
        d_bt = [dmasem(f"d_bt{i}") for i in range(4)]
        d_xres = [dmasem("d_xr0"), dmasem("d_xr1")]
        d_out = [dmasem("d_out0"), dmasem("d_out1")]

        dummy = Tk()
        op("pool", lambda: POOL.dma_start(out=identb[:], in_=identd[:, :]), dma=d_cp)
        op("pool", lambda: POOL.dma_start(out=prevb[:], in_=prevd[:, :]), dma=d_cp)
        op("pool", lambda: POOL.dma_start(out=cmb[:], in_=cmd[:, :]), dma=d_cp)
        op("sp", lambda: SP.dma_start(out=normw[:], in_=normw_t[:, :]), dma=d_c)
        for gi in range(4):
            op("sp", lambda gi=gi: SP.dma_start(out=gt[:, gi, :], in_=gains[gi:gi + 1, :].partition_broadcast(128)), dma=d_c)
        for e_ in ("dve", "act", "pe", "pool"):
            wait(e_, (d_c, d_c.n))
            wait(e_, (d_cp, d_cp.n))
        t_c0 = Tk()
        op("dve", lambda: DVE.memset(epst[:], EPS), writes=[t_c0])
        op("dve", lambda: DVE.memset(onest[:], 1.0), writes=[t_c0])
        gsq = xf[0][:, 0:512].rearrange("p (g d) -> p g d", g=4)
        op("dve", lambda: DVE.tensor_tensor(out=gsq, in0=gt[:], in1=gt[:], op=ALU.mult), writes=[t_c0, t_xf[0]])
        op("dve", lambda: DVE.tensor_reduce(out=gmx[:], in_=gsq, axis=AX.X, op=ALU.max), reads=[t_c0, t_xf[0]], writes=[t_c0])
        op("dve", lambda: DVE.tensor_tensor(out=negc[:, 0:1], in0=gmx[:, 0:1], in1=gmx[:, 1:2], op=ALU.mult), reads=[t_c0], writes=[t_negc])
        op("dve", lambda: DVE.tensor_tensor(out=negc[:, 1:2], in0=gmx[:, 2:3], in1=gmx[:, 3:4], op=ALU.mult), reads=[t_c0], writes=[t_negc])
        op("act", lambda: ACTE.activation(out=negc[:], in_=negc[:], func=ACT.Ln, scale=128.0), reads=[t_negc], writes=[t_negc])
        op("act", lambda: ACTE.activation(out=negc[:], in_=negc[:], func=ACT.Exp, scale=0.5), reads=[t_negc], writes=[t_negc])
        op("dve", lambda: DVE.tensor_scalar(out=negc[:], in0=negc[:], scalar1=-1.0, scalar2=None, op0=ALU.mult), reads=[t_negc], writes=[t_negc])
        sc = float(HD) ** -0.5
        op("dve", lambda: DVE.tensor_scalar(out=gt[:, 0, :], in0=gt[:, 0, :], scalar1=sc, scalar2=None, op0=ALU.mult), reads=[t_c0], writes=[t_gt])
        op("dve", lambda: DVE.tensor_scalar(out=gt[:, 2, :], in0=gt[:, 2, :], scalar1=sc, scalar2=None, op0=ALU.mult), reads=[t_c0], writes=[t_gt])
        op("dve", lambda: DVE.memset(VAE[:].rearrange("p a b c -> p (a b c)"), 1.0), writes=[t_VAE])
        op("dve", lambda: DVE.memset(VBE[:].rearrange("p a b c -> p (a b c)"), 1.0), writes=[t_VBE])
        for e_ in ("act", "pe", "pool"):
            wait(e_, (ctr["dve"], ctr["dve"].n))

        state = {"xi": 0, "wi": 0, "ui": 0, "mi": 0, "qi": 0, "ri": 0, "si": 0, "pi": 0, "oi": 0, "bi": 0, "yi": 0, "ki": 0, "ti": 0}
        cached = set()
        t_wbf = {}

        def gidx_in(col0):
            return col0 // 256

        deferred_store = [None]

        def load_w(g):
            i = state["wi"] % NWS
            state["wi"] += 1
            if g in cached:
                op("sp", lambda: SP.dma_start(out=ws[i][:].rearrange("p c n -> p (c n)"), in_=wbf[g, :, :]),
                   reads=[t_wbf[g]], writes=[t_ws[i]], dma=d_wss[i])
                return i
            if g < 26:
                src = w_in[:, g * 256:(g + 1) * 256].rearrange("(c p) n -> p c n", p=128)
            else:
                src = w_out[:, (g - 26) * 256:(g - 25) * 256].rearrange("(c p) n -> p c n", p=128)
            op("pool", lambda: POOL.dma_start(out=ws[i][:], in_=src), writes=[t_ws[i]], dma=d_wsp[i])
            if g < 26:
                op("dve", lambda: DVE.tensor_tensor(out=ws[i][:], in0=ws[i][:],
                                                    in1=normw[:, :].unsqueeze(2).broadcast_to([128, 16, 256]), op=ALU.mult),
                   writes=[t_ws[i]])
            t_wbf[g] = Tk()
            deferred_store[0] = (lambda: op(
                "sp", lambda: SP.dma_start(out=wbf[g, :, :], in_=ws[i][:].rearrange("p c n -> p (c n)")),
                reads=[t_ws[i]], writes=[t_wbf[g]], dma=d_wst[i]))
            cached.add(g)
            return i

        class WQ:
            def __init__(self, groups):
                self.groups = list(groups)
                self.pos = 0
                self.slots = {}
                self.pending = []
                self._issue()

            def _issue(self):
                if self.pos < len(self.groups):
                    deferred_store[0] = None
                    self.slots[self.pos] = load_w(self.groups[self.pos])
                    if deferred_store[0] is not None:
                        self.pending.append((self.pos, deferred_store[0]))
                    self.pos += 1

            def flush(self, below=None):
                keep = []
                for qi, fn in self.pending:
                    if below is None or qi < below:
                        fn()
                    else:
                        keep.append((qi, fn))
                self.pending = keep

            def get(self, k):
                self.flush(below=k)
                while self.pos <= k + NWS - 1 and self.pos < len(self.groups):
                    self._issue()
                return self.slots[k]

        def gen_pipe(items, stages):
            n = len(items)
            maxd = max(d for d, _ in stages)
            for s_ in range(n + maxd):
                for d_, fn in stages:
                    i = s_ - d_
                    if 0 <= i < n:
                        fn(items[i])
                yield

        def run_pipe(items, stages):
            for _ in gen_pipe(items, stages):
                pass

        def nt_load(item):
            item["xi"] = state["xi"] % 2
            state["xi"] += 1
            i = item["xi"]
            op("sp", lambda: SP.dma_start(out=xf[i][:], in_=item["src"]), writes=[t_xf[i]], dma=d_xf[i])

        def nt_norm(item):
            i = item["xi"]
            rs, t_rs = item["rs"]
            op("dve", lambda: DVE.tensor_copy(out=xb[i][:], in_=xf[i][:]), reads=[t_xf[i]], writes=[t_xb[i]])
            op("act", lambda: ACTE.activation(out=xf[i][:], in_=xf[i][:], func=ACT.Square, accum_out=rs[:, 0:1]),
               writes=[t_xf[i], t_rs])
            op("act", lambda: ACTE.activation(out=rs[:, 1:2], in_=rs[:, 0:1], func=ACT.Ln, scale=1.0 / D, bias=epst[:]),
               reads=[t_rs], writes=[t_rs])
            op("act", lambda: ACTE.activation(out=rs[:, 2:3], in_=rs[:, 1:2], func=ACT.Exp, scale=-0.5),
               reads=[t_rs], writes=[t_rs])
            op("act", lambda: ACTE.mul(out=rs[:, 3:4], in_=rs[:, 2:3], mul=-1.0), reads=[t_rs], writes=[t_rs])

        def nt_tr(item):
            i = item["xi"]
            dst = item["dst"]
            alt = state["ti"] % 2
            state["ti"] += 1
            for hf in range(2):
                tk_ = t_pst(hf, alt)
                for c8 in range(8):
                    c = hf * 8 + c8
                    op("pe", lambda c=c, hf=hf, c8=c8: PE.transpose(out=pst(hf, alt)[:, c8, :], in_=xb[i][:, c * 128:(c + 1) * 128],
                                                                   identity=identb[:]),
                       reads=[t_xb[i]] if c8 == 0 else [], writes=[tk_] if c8 == 0 else [], inc=(c8 == 7))
                mark(pe_ev(), reads=[t_xb[i]], writes=[tk_])
                if hf == 0:
                    op("dve", lambda: DVE.tensor_copy(out=dst[:, 0:8, :], in_=pst(0, alt)), reads=[tk_], writes=[item["t_dst"][0]])
                else:
                    op("dve", lambda: DVE.tensor_copy(out=dst[:, 8:16, :], in_=pst(1, alt)), reads=[tk_], writes=[item["t_dst"][1]])

        def proj(xT_fn, t_x, wslot, pu):
            txl = list(t_x) if isinstance(t_x, (list, tuple)) else [t_x]
            for c in range(16):
                op("pe", lambda c=c: PE.matmul(psu(pu), lhsT=xT_fn(c), rhs=ws[wslot][:, c, :], start=(c == 0), stop=(c == 15)),
                   reads=txl + [t_ws[wslot]] if c == 0 else [], writes=[t_pu(pu)] if c == 0 else [], inc=(c == 15))
            mark(pe_ev(), reads=txl + [t_ws[wslot]], writes=[t_pu(pu)])

        def next_pu():
            if state.get("pu_mode") == "single":
                return 2
            pu = state["ui"] % 2
            state["ui"] += 1
            return pu

        def qk_evac(item, pu):
            k = state["ki"] % 2
            state["ki"] += 1
            item["k"] = k
            rs, t_rs = item["rs"]
            op("dve", lambda: DVE.tensor_scalar(out=wk[k][:], in0=psu(pu), scalar1=rs[:, 2:3], scalar2=None, op0=ALU.mult),
               reads=[t_pu(pu), t_rs], writes=[t_wk[k]])

        def qk_stats(item):
            k = item["k"]
            gidx = item["gidx"]
            for h in range(2):
                op("act", lambda h=h: ACTE.activation(out=wt2[k][:, h * 128:(h + 1) * 128], in_=wk[k][:, h * 128:(h + 1) * 128],
                                                      func=ACT.Square, accum_out=wst[k][:, h:h + 1]),
                   reads=[t_wk[k]], writes=[t_wt2[k], t_wst[k]])
            op("act", lambda: ACTE.activation(out=wst[k][:, 2:4], in_=wst[k][:, 0:2], func=ACT.Ln, scale=1.0 / HD, bias=epst[:]),
               reads=[t_wst[k]], writes=[t_wst[k]])
            op("act", lambda: ACTE.activation(out=wst[k][:, 4:6], in_=wst[k][:, 2:4], func=ACT.Exp, scale=-0.5),
               reads=[t_wst[k]], writes=[t_wst[k]])
            gb = gt[:, gidx, :].unsqueeze(1).broadcast_to([128, 2, 128])
            v3 = lambda t_: t_[:].rearrange("p (h d) -> p h d", h=2)
            op("pool", lambda: POOL.tensor_tensor(out=v3(wxg[k]), in0=v3(wk[k]), in1=gb, op=ALU.mult),
               reads=[t_wk[k], t_gt], writes=[t_wxg[k]])
            item["cur"] = (wxg[k], t_wxg[k])
            r = item.get("rope")
            if r is not None:
                rt = rope_t[r]
                Cb = rt[:, 0, :].unsqueeze(1).broadcast_to([128, 2, 128])
                op("pool", lambda: POOL.tensor_tensor(out=v3(wt1[k]), in0=v3(wxg[k]), in1=Cb, op=ALU.mult),
                   reads=[t_wxg[k], t_rope[r]], writes=[t_wt1[k]])
                xv = wxg[k][:].rearrange("p (a s j) -> p a s j", a=4, s=2)
                ov = wt2[k][:].rearrange("p (a s j) -> p a s j", a=4, s=2)
                sv = rt[:, 1, :].rearrange("p (a s j) -> p a s j", a=2, s=2)
                first = True
                for hh in range(2):
                    for s_ in range(2):
                        op("pool", lambda hh=hh, s_=s_: POOL.tensor_tensor(
                            out=ov[:, 2 * hh:2 * hh + 2, s_, :], in0=xv[:, 2 * hh:2 * hh + 2, 1 - s_, :],
                            in1=sv[:, :, s_, :], op=ALU.mult),
                           reads=[t_wxg[k], t_rope[r]] if first else [], writes=[t_wt2[k]] if first else [])
                        first = False
                mark((ctr["pool"], ctr["pool"].n), reads=[t_wxg[k], t_rope[r]], writes=[t_wt2[k]])
                op("pool", lambda: POOL.tensor_tensor(out=wt1[k][:], in0=wt1[k][:], in1=wt2[k][:], op=ALU.add),
                   reads=[t_wt1[k], t_wt2[k]], writes=[t_wt1[k]])
                item["cur"] = (wt1[k], t_wt1[k])

        def qk_final(item):
            k = item["k"]
            cur, t_cur = item["cur"]
            q_ = state["qi"] % 2
            state["qi"] += 1
            rb = wst[k][:, 4:6].unsqueeze(2).broadcast_to([128, 2, 128])
            op("dve", lambda: DVE.tensor_tensor(out=qtok[q_][:].rearrange("p (h d) -> p h d", h=2),
                                                in0=cur[:].rearrange("p (h d) -> p h d", h=2), in1=rb, op=ALU.mult),
               reads=[t_cur, t_wst[k]], writes=[t_qtok[q_]])
            m = 1
            for h in range(2):
                op("pe", lambda h=h: PE.transpose(out=ps_m[:, m, h * 128:(h + 1) * 128], in_=qtok[q_][:, h * 128:(h + 1) * 128],
                                                  identity=identb[:]),
                   reads=[t_qtok[q_]] if h == 0 else [], writes=[t_ps_m[m]] if h == 0 else [], inc=(h == 1))
            mark(pe_ev(), reads=[t_qtok[q_]], writes=[t_ps_m[m]])
            ncols = item.get("ncols", 128)
            dst_fn, t_dst = item["dst_fn"], item["t_dst"]
            for h in range(2):
                op("dve", lambda h=h: DVE.tensor_copy(out=dst_fn(h), in_=ps_m[:, m, h * 128:h * 128 + ncols]),
                   reads=[t_ps_m[m]], writes=[t_dst])

        def load_rope(item, src_ap):
            r = state["ri"] % 2
            state["ri"] += 1
            item["rope"] = r
            op("sp", lambda: SP.dma_start(out=rope_t[r][:], in_=src_ap), writes=[t_rope[r]], dma=d_rope[r])

        def fin_dve(item):
            j = item["acc"]
            a = state["oi"] % 2
            state["oi"] += 1
            item["og"] = a
            acc = pso(j)
            op("dve", lambda: DVE.reciprocal(out=rl[a][:], in_=acc[:, 128:129]), reads=[t_ps_o[j // 2]], writes=[t_rl[a]])
            op("dve", lambda: DVE.scalar_tensor_tensor(out=og[a][:], in0=acc[:, 0:128], scalar=rl[a][:, 0:1], in1=item["zsrc"],
                                                       op0=ALU.mult, op1=ALU.mult),
               reads=[t_ps_o[j // 2], t_rl[a], item.get("t_z", t_ZG)], writes=[t_og[a]])

        def fin_tr(item):
            a = item["og"]
            m = 1
            chunk, tokcol = item["och"], item["otok"]
            op("pe", lambda: PE.transpose(out=ps_m[:, m, 0:128], in_=og[a][:], identity=identb[:]),
               reads=[t_og[a]], writes=[t_ps_m[m]])
            op("dve", lambda: DVE.tensor_copy(out=OT[:, chunk, tokcol:tokcol + 128], in_=ps_m[:, m, 0:128]),
               reads=[t_ps_m[m]], writes=[t_OT])

        class _Stop(Exception):
            pass
        STOP = os.environ.get("KSTOP", "")

        def stop(name):
            if STOP == name:
                raise _Stop()

        def main():
            chunk_id = 0
            for (nt, tag) in SEQS:
                nkc = nt + 1
                wq = WQ([gidx_in(C_KA), gidx_in(C_VA)])
                wka, wva = wq.get(0), wq.get(1)
                items = []
                for t in range(nt + 1):
                    meta = (t == nt)
                    items.append({
                        "t": t, "npart": 16 if meta else 128, "ncols": 16 if meta else 128,
                        "src": xloc[0, 8 * 128:9 * 128, :] if meta else xall[tag][t * 128:(t + 1) * 128, :],
                        "dst": xnTl[:, :, (t % 2) * 128:(t % 2 + 1) * 128], "t_dst": t_xnTl[t % 2],
                        "gidx": 1, "t_dstk": t_KAT, "rs": (st[t % 4][:, :], t_st[t % 4]),
                    })
                for it in items:
                    it["dst_fn"] = (lambda h, it=it: KAT[:, h, it["t"] * 128:it["t"] * 128 + it["npart"]])
                    it["t_dst_k"] = t_KAT

                def kv_proj(it):
                    t = it["t"]
                    o = t % 2
                    npart = it["npart"]
                    load_rope(it, ropekv[t if t < nt else 64, :, :, :])
                    pk = next_pu()
                    proj(lambda c: xnTl[:, c, o * 128:(o + 1) * 128], t_xnTl[o], wka, pk)
                    qk_evac(it, pk)
                    pv = next_pu()
                    proj(lambda c: xnTl[:, c, o * 128:(o + 1) * 128], t_xnTl[o], wva, pv)
                    rs, t_rs = it["rs"]
                    op("dve", lambda: DVE.tensor_scalar(out=VAE[:npart, t, :, 0:128],
                                                        in0=psu(pv)[:npart, :].rearrange("p (h d) -> p h d", h=2),
                                                        scalar1=rs[:npart, 2:3], scalar2=None, op0=ALU.mult),
                       reads=[t_bk[pv], t_rs], writes=[t_VAE])

                def kv_final(it):
                    it2 = dict(it)
                    it2["t_dst"] = t_KAT
                    it2["k"] = it["k"]
                    it2["cur"] = it["cur"]
                    qk_final(it2)

                run_pipe(items, [(0, nt_load), (1, nt_norm), (2, nt_tr), (3, kv_proj), (4, qk_stats), (5, kv_final)])
                wq.flush()
                stop("kv")
                for q in range(2):
                    ck = chunk_id
                    chunk_id += 1
                    LT_ORDER = [2, 3, 4, 5, 0, 1, 6, 7, 8]
                    nitems = [{"src": xloc[ck, lt * 128:(lt + 1) * 128, :], "dst": xnTl[:, :, lt * 128:(lt + 1) * 128],
                               "t_dst": t_xnTl[lt], "rs": (rloc[:, lt, :], t_rloc[lt])} for lt in LT_ORDER]
                    gnorm = gen_pipe(nitems, [(0, nt_load), (1, nt_norm), (2, nt_tr)])
                    groups = []
                    bgr = lambda hp: [gidx_in(C_QB) + hp, gidx_in(C_KB) + hp, gidx_in(C_VB) + hp, gidx_in(C_ZB) + hp]
                    groups += bgr(0)
                    for hp in range(4):
                        groups += [gidx_in(C_QA) + hp, gidx_in(C_ZA) + hp]
                        if hp < 3:
                            groups += bgr(hp + 1)
                    groups += [26 + n for n in range(8)]
                    wq = None
                    gpos = [0]

                    def nextw():
                        s_ = wq.get(gpos[0])
                        gpos[0] += 1
                        return s_

                    def xl(lt):
                        return lambda c, lt=lt: xnTl[:, c, lt * 128:(lt + 1) * 128]

                    def ip_proj(it):
                        lt = it["lt"]
                        it["rs"] = (rloc[:, lt, :], t_rloc[lt])
                        rs, t_rs = it["rs"]
                        pu = next_pu()
                        if it["kind"] == "qa":
                            load_rope(it, ropeq[ck * 4 + it["j"], :, :, :])
                        proj(xl(lt), t_xnTl[lt], it["w"], pu)
                        kind = it["kind"]
                        if kind in ("qb", "kb", "qa"):
                            qk_evac(it, pu)
                        elif kind == "vb":
                            op("dve", lambda: DVE.tensor_scalar(out=VBE[:, lt, :, 0:128],
                                                                in0=psu(pu).rearrange("p (h d) -> p h d", h=2),
                                                                scalar1=rs[:, 2:3], scalar2=None, op0=ALU.mult),
                               reads=[t_pu(pu), t_rs], writes=[t_VBE])
                        else:
                            zdst, t_zdst = (ZG, t_ZG) if kind == "zb" else (it["zga"], it["t_zga"])
                            k = state["ki"] % 2
                            state["ki"] += 1
                            op("act", lambda: ACTE.activation(out=wk[k][:], in_=psu(pu), func=ACT.Exp, scale=rs[:, 3:4]),
                               reads=[t_pu(pu), t_rs], writes=[t_wk[k]])
                            op("act", lambda: ACTE.activation(out=wk[k][:], in_=wk[k][:], func=ACT.Ln, bias=onest[:]),
                               reads=[t_wk[k]], writes=[t_wk[k]])
                            op("act", lambda: ACTE.activation(out=wk[k][:], in_=wk[k][:], func=ACT.Exp, scale=-1.0),
                               reads=[t_wk[k]], writes=[t_wk[k]])
                            op("dve", lambda: DVE.scalar_tensor_tensor(out=zdst[:, it["j"], :], in0=psu(pu), scalar=rs[:, 2:3],
                                                                       in1=wk[k][:], op0=ALU.mult, op1=ALU.mult),
                               reads=[t_pu(pu), t_wk[k], t_rs], writes=[t_zdst])

                    def ip_stats(it):
                        if it["kind"] in ("qb", "kb", "qa"):
                            qk_stats(it)

                    def ip_final(it):
                        if it["kind"] in ("qb", "kb", "qa"):
                            qk_final(it)

                    def gen_bproj(hp):
                        wq_, wk_, wv_, wz_ = nextw(), None, None, None
                        for b in range(4):
                            bi = b
                            for qa in range(2):
                                base = ((((ck * 4 + b) * 2 + qa) * 8 + hp * 2) * 10) * 128
                                src = AP(rw.tensor, base, [[1, 64], [10 * 128, 2], [128, 10], [1, 64]])
                                dst = BT[bi][qa * 64:(qa + 1) * 64, :, :].rearrange("p h (r c) -> p h r c", r=10)
                                op("pool", lambda src=src, dst=dst: POOL.dma_start(out=dst, in_=src),
                                   writes=[t_BT[bi]] if qa == 0 else [], dma=d_bt[bi])
                            t_BT[bi].w = (d_bt[bi], d_bt[bi].n)
                            op("pool", lambda bi=bi: POOL.tensor_tensor(
                                out=BT[bi][:].rearrange("p h (r c) -> p (h r) c", r=10),
                                in0=BT[bi][:].rearrange("p h (r c) -> p (h r) c", r=10),
                                in1=cmb[:, :].unsqueeze(1).broadcast_to([128, 20, 64]), op=ALU.add),
                               writes=[t_BT[bi]])
                        items = []
                        for j in range(4):
                            items.append({"kind": "qb", "lt": 2 + j, "j": j, "w": wq_, "gidx": 2, "t_dst": t_QT,
                                          "dst_fn": (lambda h, j=j: QT[:, h, j * 128:(j + 1) * 128])})
                        yield from gen_pipe(items, [(3, ip_final), (0, ip_proj), (1, ip_stats)])
                        wk_ = nextw()
                        items = []
                        for lt in LT_ORDER:
                            items.append({"kind": "kb", "lt": lt, "w": wk_, "gidx": 3, "t_dst": t_KBT,
                                          "dst_fn": (lambda h, lt=lt: KBT[:, h, lt * 128:(lt + 1) * 128])})
                        yield from gen_pipe(items, [(3, ip_final), (0, ip_proj), (1, ip_stats)])
                        wv_ = nextw()
                        items = [{"kind": "vb", "lt": lt, "w": wv_} for lt in LT_ORDER]
                        yield from gen_pipe(items, [(0, ip_proj)])
                        wz_ = nextw()
                        items = [{"kind": "zb", "lt": 2 + j, "j": j, "w": wz_} for j in range(4)]
                        yield from gen_pipe(items, [(0, ip_proj)])

                    def battn(hp):
                        items = []
                        for b in range(4):
                            for hl in range(2):
                                items.append({"b": b, "hl": hl, "zsrc": ZG[:, b, hl * 128:(hl + 1) * 128],
                                              "och": 8 + hp * 2 + hl, "otok": b * 128})

                        def b_bt(it):
                            it["bt"] = it["b"]

                        def b_scores(it):
                            b, hl = it["b"], it["hl"]
                            if hl == 1:
                                it["bt"] = items[items.index(it) - 1]["bt"]
                            bi = it["bt"]
                            sp_ = state["si"] % 2
                            state["si"] += 1
                            it["sp"] = sp_
                            B0, B1 = 2 * sp_, 2 * sp_ + 1
                            for kt in range(5):
                                bank = B0 if kt < 4 else B1
                                outp = psA[:, bank, (kt % 4) * 128:(kt % 4) * 128 + 128]
                                first = (kt == 0)
                                op("pe", lambda kt=kt, outp=outp: PE.matmul(
                                    outp, lhsT=KBT[:, hl, (b + kt) * 128:(b + kt + 1) * 128], rhs=QT[:, hl, b * 128:(b + 1) * 128],
                                    start=True, stop=False),
                                   reads=[t_KBT, t_QT] if first else [], writes=[t_bk[B0], t_bk[B1]] if first else [], inc=False)
                                op("pe", lambda kt=kt, outp=outp: PE.matmul(
                                    outp, lhsT=BT[bi][:, hl, kt * 128:(kt + 1) * 128], rhs=prevb[:], start=False, stop=True),
                                   reads=[t_BT[bi]] if first else [], inc=False)
                            op("pe", lambda: PE.matmul(psA[0:16, B1, 128:256], lhsT=KBT[:, hl, 8 * 128:8 * 128 + 16],
                                                       rhs=QT[:, hl, b * 128:(b + 1) * 128], start=True, stop=True))
                            mark(pe_ev(), reads=[t_KBT, t_QT, t_BT[bi]], writes=[t_bk[B0], t_bk[B1]])
                            p_ = state["pi"] % 2
                            state["pi"] += 1
                            it["pt"] = p_
                            op("act", lambda: ACTE.activation(out=PT[p_][:, 0:512], in_=psA[:, B0, :], func=ACT.Exp, bias=negc[:, 1:2]),
                               reads=[t_bk[B0], t_negc], writes=[t_PT[p_]])
                            op("act", lambda: ACTE.activation(out=PT[p_][:, 512:640], in_=psA[:, B1, 0:128], func=ACT.Exp, bias=negc[:, 1:2]),
                               reads=[t_bk[B1]], writes=[t_PT[p_]])
                            op("act", lambda: ACTE.activation(out=PT[p_][0:16, 640:768], in_=psA[0:16, B1, 128:256], func=ACT.Exp,
                                                              bias=negc[0:16, 1:2]),
                               reads=[t_bk[B1]], writes=[t_PT[p_]])

                        def b_pv(it):
                            b, hl, p_ = it["b"], it["hl"], it["pt"]
                            j = (state["oi2"] % 2) * 2 if "oi2" in state else 0
                            state["oi2"] = state.get("oi2", 0) + 1
                            it["acc"] = j
                            for kt in range(5):
                                op("pe", lambda kt=kt: PE.matmul(
                                    pso(j), lhsT=PT[p_][:, kt * 128:(kt + 1) * 128], rhs=VBE[:, b + kt, hl, 0:129],
                                    start=(kt == 0), stop=False),
                                   reads=[t_PT[p_], t_VBE] if kt == 0 else [], writes=[t_ps_o[j // 2]] if kt == 0 else [], inc=False)
                            op("pe", lambda: PE.matmul(pso(j), lhsT=PT[p_][0:16, 640:768], rhs=VBE[0:16, 8, hl, 0:129],
                                                       start=False, stop=True))
                            mark(pe_ev(), reads=[t_PT[p_], t_VBE], writes=[t_ps_o[j // 2]])
                            fin_dve(it)

                        n_it = len(items)
                        for s_ in range(-2, n_it + 2):
                            if 0 <= s_ + 2 < n_it:
                                b_bt(items[s_ + 2])
                            if 0 <= s_ < n_it:
                                b_scores(items[s_])
                            if 0 <= s_ - 1 < n_it:
                                b_pv(items[s_ - 1])
                            if 0 <= s_ - 2 < n_it:
                                fin_tr(items[s_ - 2])

                    def gen_aproj(hp):
                        QTA, t_QTA, ZGA, t_ZGA = QTAs[hp % 2], t_QTAs[hp % 2], ZGAs[hp % 2], t_ZGAs[hp % 2]
                        wq_ = nextw()
                        items = []
                        for j in range(4):
                            items.append({"kind": "qa", "lt": 2 + j, "j": j, "w": wq_, "gidx": 0, "t_dst": t_QTA,
                                          "dst_fn": (lambda h, j=j: QTA[:, h, j * 128:(j + 1) * 128])})
                        yield from gen_pipe(items, [(3, ip_final), (0, ip_proj), (1, ip_stats)])
                        wz_ = nextw()
                        items = [{"kind": "za", "lt": 2 + j, "j": j, "w": wz_, "zga": ZGA, "t_zga": t_ZGA} for j in range(4)]
                        yield from gen_pipe(items, [(0, ip_proj)])

                    def gen_next(hp):
                        yield from gen_bproj(hp)
                        yield from gen_aproj(hp)

                    def aphase(hp, genB, nsteps):
                        g = hp // 2
                        QTA, t_QTA, ZGA, t_ZGA = QTAs[hp % 2], t_QTAs[hp % 2], ZGAs[hp % 2], t_ZGAs[hp % 2]
                        if genB is not None:
                            state["pu_mode"] = "single"
                        ucount = [0]
                        done_steps = [0]
                        nunits_tot = 2 * (nt // 2 + 1)
                        for hl in range(2):
                            units = [[2 * p, 2 * p + 1] for p in range(nt // 2)] + [[nt]]

                            def S(u):
                                cs = units[u]
                                sp_ = u % 2
                                p_ = state["pi"] % 2
                                state["pi"] += 1
                                if len(cs) == 2:
                                    for i_, c in enumerate(cs):
                                        bank = 2 * sp_ + i_
                                        op("pe", lambda c=c, bank=bank: PE.matmul(psA[:, bank, :], lhsT=KAT[:, g, c * 128:(c + 1) * 128],
                                                                                  rhs=QTA[:, hl, :], start=True, stop=True),
                                           reads=[t_KAT, t_QTA] if i_ == 0 else [],
                                           writes=[t_bk[2 * sp_], t_bk[2 * sp_ + 1]] if i_ == 0 else [], inc=(i_ == 1))
                                    mark(pe_ev(), reads=[t_KAT, t_QTA], writes=[t_bk[2 * sp_], t_bk[2 * sp_ + 1]])
                                    op("act", lambda: ACTE.activation(out=PT[p_][:].rearrange("p (a n) -> p a n", a=2),
                                                                      in_=psA[:, 2 * sp_:2 * sp_ + 2, :], func=ACT.Exp, bias=negc[:, 0:1]),
                                       reads=[t_bk[2 * sp_], t_bk[2 * sp_ + 1], t_negc], writes=[t_PT[p_]])
                                else:
                                    c = cs[0]
                                    bank = 2 * sp_
                                    op("pe", lambda: PE.matmul(psA[:16, bank, :], lhsT=KAT[:, g, c * 128:c * 128 + 16], rhs=QTA[:, hl, :],
                                                               start=True, stop=True),
                                       reads=[t_KAT, t_QTA], writes=[t_bk[bank]])
                                    op("act", lambda: ACTE.activation(out=PT[p_][:16, 0:512], in_=psA[:16, bank, :], func=ACT.Exp,
                                                                      bias=negc[:16, 0:1]),
                                       reads=[t_bk[bank], t_negc], writes=[t_PT[p_]])
                                return p_

                            def PV(u, p_):
                                cs = units[u]
                                last_u = (u == len(units) - 1)
                                for i_, c in enumerate(cs):
                                    kc = 128 if c < nt else 16
                                    for j in range(4):
                                        firstmm = (u == 0 and i_ == 0)
                                        lastmm = (last_u and i_ == len(cs) - 1)
                                        op("pe", lambda j=j, c=c, i_=i_, kc=kc: PE.matmul(
                                            pso(j), lhsT=PT[p_][:kc, i_ * 512 + j * 128:i_ * 512 + (j + 1) * 128],
                                            rhs=VAE[:kc, c, g, 0:129], start=(firstmm and j % 2 == 0), stop=lastmm,
                                            skip_group_check=True),
                                           reads=[t_PT[p_], t_VAE] if (i_ == 0 and j == 0) else [],
                                           writes=[t_ps_o[0], t_ps_o[1]] if (firstmm and j == 0) else [],
                                           inc=(i_ == len(cs) - 1 and j == 3))
                                mark(pe_ev(), reads=[t_PT[p_], t_VAE], writes=[t_ps_o[0], t_ps_o[1]] if last_u else [])

                            pend = S(0)
                            for u in range(len(units)):
                                nxt = S(u + 1) if u + 1 < len(units) else None
                                PV(u, pend)
                                pend = nxt
                                ucount[0] += 1
                                if genB is not None:
                                    want = min((ucount[0] * nsteps) // nunits_tot, done_steps[0] + 1)
                                    while done_steps[0] < want:
                                        next(genB, None)
                                        done_steps[0] += 1
                            fitems = [{"acc": j, "zsrc": ZGA[:, j, hl * 128:(hl + 1) * 128], "t_z": t_ZGA,
                                       "och": hp * 2 + hl, "otok": j * 128}
                                      for j in range(4)]
                            run_pipe(fitems, [(0, fin_dve), (1, fin_tr)])

                    g0 = gen_next(0)
                    step_ = 0
                    for _ in gnorm:
                        if step_ == 3:
                            wq = WQ(groups)
                        if step_ >= 4:
                            next(g0, None)
                        step_ += 1
                    for _ in g0:
                        pass
                    for hp in range(4):
                        battn(hp)
                        genB = gen_next(hp + 1) if hp < 3 else None
                        aphase(hp, genB, 46)
                        state["pu_mode"] = "double"
                        if genB is not None:
                            for _ in genB:
                                pass
                    stop("A")
                    for n in range(8):
                        wo = nextw()
                        for j in range(4):
                            a = state["yi"] % 2
                            state["yi"] += 1
                            op("sp", lambda: SP.dma_start(out=xres[a][:], in_=xloc[ck, (2 + j) * 128:(3 + j) * 128, n * 256:(n + 1) * 256]),
                               writes=[t_xres[a]], dma=d_xres[a])
                            pu = next_pu()
                            proj(lambda c, j=j: OT[:, c, j * 128:(j + 1) * 128], t_OT, wo, pu)
                            op("dve", lambda: DVE.tensor_tensor(out=ysb[a][:], in0=psu(pu), in1=xres[a][:], op=ALU.add),
                               reads=[t_bk[pu], t_xres[a]], writes=[t_ysb[a]])
                            op("act", lambda: ACTE.dma_start(out=yout[ck, j * 128:(j + 1) * 128, n * 256:(n + 1) * 256], in_=ysb[a][:]),
                               reads=[t_ysb[a]], dma=d_out[a])
                    wq.flush()
        try:
            main()
        except _Stop:
            pass
        SP.wait_ge(d_out[0].h, d_out[0].n)
        SP.wait_ge(d_out[1].h, d_out[1].n)
    return nc, used


def _rope_tables(rows, cols):
    half = 64
    inv = (10000.0 ** (-np.arange(0, half, 2, dtype=np.float32) / half)).astype(np.float32)
    ar = rows.astype(np.float32)[:, None] * inv[None, :]
    ac = cols.astype(np.float32)[:, None] * inv[None, :]
    cr, sr, cc, sc = np.cos(ar), np.sin(ar), np.cos(ac), np.sin(ac)
    C = np.concatenate([cr, cr, cc, cc], axis=1)
    S = np.concatenate([-sr, sr, -sc, sc], axis=1)
    return np.stack([C, S], axis=1).astype(np.float32)


def _tile_rope(T):
    p = np.arange(128)
    return _rope_tables(2 * T + p // 64, p % 64)


def _meta_rope():
    r = np.zeros((128,), np.int64) - 1
    c = np.arange(128) % 16
    return _rope_tables(r, c)


_NC_CACHE = {}


def kernel(x_prompt, x_sample, meta_tokens, norm_w, w_in, q_norm_a, k_norm_a, q_norm_b, k_norm_b, rpb, w_out):
    f32 = np.float32
    x_prompt = np.asarray(x_prompt, f32); x_sample = np.asarray(x_sample, f32)
    meta_tokens = np.asarray(meta_tokens, f32)
    w_in2 = np.ascontiguousarray(np.asarray(w_in, f32)[0]); w_out2 = np.ascontiguousarray(np.asarray(w_out, f32)[0])
    rpb2 = np.asarray(rpb, f32)[0]
    normw_t = np.ascontiguousarray(np.asarray(norm_w, f32)[0].reshape(16, 128).T)
    gains = np.stack([np.asarray(q_norm_a, f32)[0], np.asarray(k_norm_a, f32)[0],
                      np.asarray(q_norm_b, f32)[0], np.asarray(k_norm_b, f32)[0]]).astype(f32)
    ident = np.eye(128, dtype=f32)
    prev = np.zeros((128, 128), f32)
    for qa in range(2):
        for c in range(64):
            prev[qa * 64 + c, qa * 64 + 63 - c] = 1.0
    cm = np.full((128, 64), NEG, f32)
    for qa in range(2):
        for qcc in range(64):
            qc = 63 - qcc
            c0 = min(max(qc - 8, 0), 48)
            cm[qa * 64 + qcc, c0:c0 + 16] = 0.0
    ropekv = np.stack([_tile_rope(T) for T in range(64)] + [_meta_rope()]).astype(f32)
    meta_tile = np.zeros((128, D), f32); meta_tile[:16] = meta_tokens

    in_maps = []
    for i in range(NCORES):
        si, hi = i // 2, i % 2
        chunks = [("p", x_prompt[0], 64, 8 * i), ("p", x_prompt[0], 64, 8 * i + 4),
                  ("s", x_sample[si], 16, 8 * hi), ("s", x_sample[si], 16, 8 * hi + 4)]
        xloc = np.zeros((4, 9 * 128, D), f32)
        rwa = np.full((4, 4, 2, 8, 10, 128), NEG, f32)
        ropeq = np.zeros((16, 128, 2, 128), f32)
        for ck, (tag, xs, nb, T0) in enumerate(chunks):
            rows = 2 * nb
            gt_of = [None] * 8
            is_copy = [False] * 8
            for L in range(8):
                G = T0 - 2 + L
                if 0 <= G < nb:
                    gt_of[L] = G
            if T0 == 0:
                gt_of[1] = 3; is_copy[1] = True
            if T0 + 4 == nb:
                gt_of[6] = nb - 4; is_copy[6] = True
            for L in range(8):
                if gt_of[L] is not None:
                    xloc[ck, L * 128:(L + 1) * 128] = xs[gt_of[L] * 128:(gt_of[L] + 1) * 128]
            xloc[ck, 8 * 128:9 * 128] = meta_tile
            for b in range(4):
                Tb = T0 + b
                ropeq[ck * 4 + b] = ropekv[Tb]
                slots = list(range(b, b + 5))
                originals = {gt_of[L] for L in slots if gt_of[L] is not None and not is_copy[L]}
                for qa in range(2):
                    qr = 2 * Tb + qa
                    r0 = min(max(qr - 4, 0), rows - 8)
                    for kr_rel in range(10):
                        L = b + kr_rel // 2
                        G = gt_of[L]
                        if G is None:
                            continue
                        if is_copy[L] and G in originals:
                            continue
                        kr = 2 * G + kr_rel % 2
                        if r0 <= kr < r0 + 8:
                            dr = kr - qr + 7
                            rwa[ck, b, qa, :, kr_rel, 48:79] = rpb2[:, dr, :]
        in_maps.append({
            "xall_p": x_prompt[0], "xall_s": x_sample[si], "xloc": xloc, "ropekv": ropekv, "ropeq": ropeq,
            "w_in": w_in2, "w_out": w_out2, "normw_t": normw_t, "gains": gains, "identd": ident, "prevd": prev,
            "rw": rwa, "cmd": cm,
        })
    if "nc" not in _NC_CACHE:
        _NC_CACHE["nc"] = build_program()
    nc = _NC_CACHE["nc"]
    res = run_bass_kernel_spmd(nc, in_maps, core_ids=list(range(NCORES)))
    y_prompt = np.zeros((1, 8192, D), f32)
    y_sample = np.zeros((4, 2048, D), f32)
    for i in range(NCORES):
        y = np.asarray(res.results[i]["y"], f32)
        si, hi = i // 2, i % 2
        y_prompt[0, 1024 * i:1024 * i + 512] = y[0]
        y_prompt[0, 1024 * i + 512:1024 * (i + 1)] = y[1]
        y_sample[si, 1024 * hi:1024 * hi + 512] = y[2]
        y_sample[si, 1024 * hi + 512:1024 * (hi + 1)] = y[3]
    return (y_prompt, y_sample)
```
